# Optimizing a Trainium2 kernel written in Bass

```python
import jax
import jax.numpy as jnp
from jax import lax
import numpy as np

D_MODEL = 1024
BATCH = 16
SEQ = 256
DEPTH = 2
DEC_BATCH = 4
DEC_SEQ = 1024
PAST_LEN = 512

GRID_W = 64
EPS = 1e-6
ROPE_THETA = 10000.0
Q_BLOCK = 128
N_BRANCH = 4
BRANCH_W = 512

HG_HEADS = 4
HG_DK = 128
HG_DV = 128
HG_CHUNK = 16

GQ_HEADS = 8
GQ_KV_HEADS = 2
GQ_HD = 64

NA_HEADS = 8
NA_HD = 64
NA_WIN_R = 8
NA_WIN_C = 16

ML_HEADS = 8
ML_NOPE = 64
ML_ROPE = 32
ML_V = 64
ML_Q_RANK = 256
ML_KV_RANK = 128

FFN_HIDDEN = -(-(8 * D_MODEL) // (3 * 256)) * 256

GQ_SCALE = GQ_HD ** -0.5
NA_SCALE = NA_HD ** -0.5
ML_SCALE = (ML_NOPE + ML_ROPE) ** -0.5

IN_SPLITS = (
    HG_HEADS * HG_DK,
    HG_HEADS * HG_DK,
    HG_HEADS * HG_DK,
    HG_HEADS * HG_DV,
    HG_HEADS * HG_DV,
    GQ_HEADS * GQ_HD,
    GQ_KV_HEADS * GQ_HD,
    GQ_KV_HEADS * GQ_HD,
    NA_HEADS * NA_HD,
    NA_HEADS * NA_HD,
    NA_HEADS * NA_HD,
    ML_Q_RANK,
    ML_KV_RANK,
    ML_ROPE,
    N_BRANCH * D_MODEL,
)
IN_WIDTH = sum(IN_SPLITS)

kernel_name = 'hybrid_diffusion_prefix_trunk_step'


def rms_norm(x, gain=None):
    xf = x.astype(jnp.float32)
    y = xf * lax.rsqrt(jnp.mean(xf * xf, axis=-1, keepdims=True) + EPS)
    if gain is not None:
        y = y * gain.astype(jnp.float32)
    return y.astype(x.dtype)


def adaln(cond, w_ada, b_ada):
    m = jax.nn.silu(cond) @ w_ada + b_ada
    return [t[:, None, :] for t in jnp.split(m, 6, axis=-1)]


def axial_rope(n_tokens, rot_dim):
    n_freq = rot_dim // 4
    inv_freq = ROPE_THETA ** (-jnp.arange(n_freq, dtype=jnp.float32) / n_freq)
    t = jnp.arange(n_tokens)
    row = (t // GRID_W).astype(jnp.float32)[:, None]
    col = (t % GRID_W).astype(jnp.float32)[:, None]
    ang = jnp.concatenate([row * inv_freq, col * inv_freq], axis=-1)
    return jnp.cos(ang), jnp.sin(ang)


def apply_rope(x, cos, sin):
    xf = x.astype(jnp.float32).reshape(x.shape[:-1] + (x.shape[-1] // 2, 2))
    x1, x2 = xf[..., 0], xf[..., 1]
    c, s = cos[:, None, :], sin[:, None, :]
    return jnp.stack([x1 * c - x2 * s, x1 * s + x2 * c], axis=-1).reshape(x.shape).astype(x.dtype)


def blocked_attention(q, k, v, scale):
    B, Tq, H, dk = q.shape
    kvh = k.shape[2]
    g = H // kvh
    nb = Tq // Q_BLOCK
    qb = jnp.moveaxis(q.reshape(B, nb, Q_BLOCK, kvh, g, dk), 1, 0)

    def block(qblk):
        s = jnp.einsum('bqngd,bknd->bngqk', qblk, k).astype(jnp.float32) * scale
        p = jax.nn.softmax(s, axis=-1).astype(v.dtype)
        return jnp.einsum('bngqk,bknd->bqngd', p, v)

    o = lax.map(block, qb)
    return jnp.moveaxis(o, 0, 1).reshape(B, Tq, H, v.shape[-1])


def chunk_gla(q, k, v, log_f, s0):
    B, T, H, DK = q.shape
    DV = v.shape[-1]
    C = HG_CHUNK
    n = T // C

    def chunks(a):
        return a.astype(jnp.float32).reshape(B, n, C, H, a.shape[-1]).transpose(1, 0, 3, 2, 4)

    qc, kc, vc, gc = chunks(q), chunks(k), chunks(v), chunks(log_f)
    b = jnp.cumsum(gc, axis=-2)
    causal = jnp.tril(jnp.ones((C, C), dtype=bool))
    diff = b[..., :, None, :] - b[..., None, :, :]
    decay = jnp.where(causal[:, :, None], jnp.exp(jnp.minimum(diff, 0.0)), 0.0)
    a_intra = jnp.einsum('nbhtd,nbhtsd,nbhsd->nbhts', qc, decay, kc)
    o_intra = jnp.einsum('nbhts,nbhsv->nbhtv', a_intra, vc)
    b_last = b[..., -1:, :]
    k_to_end = kc * jnp.exp(b_last - b)
    q_from_start = qc * jnp.exp(b)
    chunk_decay = jnp.exp(b_last[..., 0, :])

    def step(S, inp):
        qs, kend, vv, dec = inp
        o_inter = jnp.einsum('bhtk,bhkv->bhtv', qs, S)
        S_new = dec[..., None] * S + jnp.einsum('bhtk,bhtv->bhkv', kend, vv)
        return S_new, o_inter

    s_final, o_inter = lax.scan(step, s0.astype(jnp.float32), (q_from_start, k_to_end, vc, chunk_decay))
    o = (o_intra + o_inter).transpose(1, 0, 3, 2, 4).reshape(B, T, H, DV)
    return o, s_final


def hgrn_mixer(hq, hff, hfb, hi, hg, lb_f, lb_b, gain, s0_f, s0_b):
    B, T, _ = hq.shape

    def heads(a):
        return a.reshape(B, T, HG_HEADS, -1)

    q = heads(hq.astype(jnp.float32)) * (HG_DK ** -0.5)
    v = heads(hi.astype(jnp.float32))

    def forget(pre, lb):
        f = lb + (1.0 - lb) * jax.nn.sigmoid(pre.astype(jnp.float32))
        return heads(1.0 - f), heads(jnp.log(f))

    k_f, lf_f = forget(hff, lb_f)
    k_b, lf_b = forget(hfb, lb_b)
    o_f, s_f = chunk_gla(q, k_f, v, lf_f, s0_f)

    def rev(a):
        return jnp.flip(a, axis=1)

    o_b, s_b = chunk_gla(rev(q), rev(k_b), rev(v), rev(lf_b), s0_b)
    o = rms_norm(o_f + rev(o_b), gain) * jax.nn.silu(heads(hg.astype(jnp.float32)))
    return o.reshape(B, T, HG_HEADS * HG_DV), s_f, s_b


def neighbourhood_attention(q, k, v, k_ctx, v_ctx, rpb):
    B, T, H, d = q.shape
    rows = T // GRID_W
    wr = min(NA_WIN_R, rows)
    r = jnp.arange(rows)
    key_rows = jnp.clip(r - wr // 2, 0, rows - wr)[:, None] + jnp.arange(wr)[None, :]
    col = jnp.arange(GRID_W)
    col_start = jnp.clip(col - NA_WIN_C // 2, 0, GRID_W - NA_WIN_C)
    col_ok = (col[None, :] >= col_start[:, None]) & (col[None, :] < col_start[:, None] + NA_WIN_C)
    qg = q.reshape(B, rows, GRID_W, H, d)
    kg = k.reshape(B, rows, GRID_W, H, d)[:, key_rows]
    vg = v.reshape(B, rows, GRID_W, H, d)[:, key_rows]
    s_loc = jnp.einsum('brqhd,brjkhd->bhrqjk', qg, kg).astype(jnp.float32) * NA_SCALE
    dr_idx = key_rows - r[:, None] + NA_WIN_R - 1
    dc_idx = jnp.clip(col[None, :] - col[:, None] + NA_WIN_C - 1, 0, 2 * NA_WIN_C - 2)
    bias = rpb.astype(jnp.float32)[:, dr_idx[:, None, :, None], dc_idx[None, :, None, :]]
    s_loc = jnp.where(col_ok[:, None, :], s_loc + bias, -jnp.inf)
    n_loc = wr * GRID_W
    s_ctx = jnp.einsum('brqhd,bphd->bhrqp', qg, k_ctx).astype(jnp.float32) * NA_SCALE
    s = jnp.concatenate([s_loc.reshape(B, H, rows, GRID_W, n_loc), s_ctx], axis=-1)
    prob = jax.nn.softmax(s, axis=-1).astype(v.dtype)
    p_loc = prob[..., :n_loc].reshape(B, H, rows, GRID_W, wr, GRID_W)
    o = (jnp.einsum('bhrqjk,brjkhd->brqhd', p_loc, vg)
         + jnp.einsum('bhrqp,bphd->brqhd', prob[..., n_loc:], v_ctx))
    return o.reshape(B, T, H, d)


def mla_keys(ckv, k_rope, w_kv_b):
    B, T, _ = ckv.shape
    kv = (ckv @ w_kv_b).reshape(B, T, ML_HEADS, ML_NOPE + ML_V)
    k = jnp.concatenate([kv[..., :ML_NOPE],
                         jnp.broadcast_to(k_rope[:, :, None, :], (B, T, ML_HEADS, ML_ROPE))], axis=-1)
    return k, kv[..., ML_NOPE:]


def merge_branches(branches, gates, w_branch, w_out):
    B, T, _ = gates.shape
    o = jnp.stack(branches, axis=2)
    bo = jnp.einsum('btnc,ncd->btnd', o, w_branch)
    g = jax.nn.sigmoid(gates.astype(jnp.float32)).reshape(B, T, N_BRANCH, D_MODEL)
    merged = jnp.sum(g * bo.astype(jnp.float32), axis=2).astype(gates.dtype)
    return merged @ w_out


def trunk_layer(x, mod, p, ctx=None):
    B, T, _ = x.shape
    sh1, sc1, g1, sh2, sc2, g2 = mod
    h = rms_norm(x) * (1 + sc1) + sh1
    split_at = np.cumsum(IN_SPLITS)[:-1].tolist()
    (hq, hff, hfb, hi, hg, gq, gk, gv, nq, nk, nv, mqa, mkva, mkr, gates) = jnp.split(h @ p['w_in'], split_at, axis=-1)
    latent = ctx is not None

    if latent:
        s0_f, s0_b = ctx['hg'][:, 0], ctx['hg'][:, 1]
    else:
        s0_f = jnp.zeros((B, HG_HEADS, HG_DK, HG_DV), jnp.float32)
        s0_b = s0_f
    out_a, s_f, s_b = hgrn_mixer(hq, hff, hfb, hi, hg, p['lb_f'], p['lb_b'], p['hg_gain'], s0_f, s0_b)

    q_b = rms_norm(gq.reshape(B, T, GQ_HEADS, GQ_HD), p['gq_q_gain'])
    k_b = rms_norm(gk.reshape(B, T, GQ_KV_HEADS, GQ_HD), p['gq_k_gain'])
    v_b = gv.reshape(B, T, GQ_KV_HEADS, GQ_HD)
    q_c = nq.reshape(B, T, NA_HEADS, NA_HD)
    k_c = nk.reshape(B, T, NA_HEADS, NA_HD)
    v_c = nv.reshape(B, T, NA_HEADS, NA_HD)
    q_d = (rms_norm(mqa, p['ml_q_a_gain']) @ p['ml_w_q_b']).reshape(B, T, ML_HEADS, ML_NOPE + ML_ROPE)
    ckv = rms_norm(mkva, p['ml_kv_a_gain'])

    if latent:
        cos_b, sin_b = axial_rope(T, GQ_HD)
        cos_d, sin_d = axial_rope(T, ML_ROPE)
        q_b = apply_rope(q_b, cos_b, sin_b)
        k_b = apply_rope(k_b, cos_b, sin_b)
        out_b = blocked_attention(q_b, jnp.concatenate([ctx['gq_k'], k_b], axis=1),
                                  jnp.concatenate([ctx['gq_v'], v_b], axis=1), GQ_SCALE)
        out_c = neighbourhood_attention(q_c, k_c, v_c, ctx['na_k'], ctx['na_v'], p['na_rpb'])
        q_d = jnp.concatenate([q_d[..., :ML_NOPE], apply_rope(q_d[..., ML_NOPE:], cos_d, sin_d)], axis=-1)
        kr = apply_rope(mkr[:, :, None, :], cos_d, sin_d)[:, :, 0]
        k_lat, v_lat = mla_keys(ckv, kr, p['ml_w_kv_b'])
        k_ctx, v_ctx = mla_keys(ctx['ml_ckv'], ctx['ml_kr'], p['ml_w_kv_b'])
        out_d = blocked_attention(q_d, jnp.concatenate([k_ctx, k_lat], axis=1),
                                  jnp.concatenate([v_ctx, v_lat], axis=1), ML_SCALE)
        new = None
    else:
        out_b = blocked_attention(q_b, k_b, v_b, GQ_SCALE)
        out_c = blocked_attention(q_c, k_c, v_c, NA_SCALE)
        k_d, v_d = mla_keys(ckv, mkr, p['ml_w_kv_b'])
        out_d = blocked_attention(q_d, k_d, v_d, ML_SCALE)
        new = (jnp.stack([s_f, s_b], axis=1), k_b, v_b, k_c, v_c, ckv, mkr)

    branches = [o.reshape(B, T, BRANCH_W).astype(x.dtype) for o in (out_a, out_b, out_c, out_d)]
    x = x + g1 * merge_branches(branches, gates, p['w_branch'], p['w_out'])
    h2 = rms_norm(x) * (1 + sc2) + sh2
    a, gate_ffn = jnp.split(h2 @ p['w_ffn_in'], 2, axis=-1)
    x = x + g2 * ((jax.nn.silu(gate_ffn) * a) @ p['w_ffn_out'])
    return x, new


def setup_inputs(seed: int = 0) -> dict:
    key = jax.random.key(seed)
    ks = jax.random.split(key, 32)

    def nrm(k, shape, scale):
        return jax.random.normal(k, shape, jnp.float32) * scale

    D = D_MODEL
    P = PAST_LEN
    return {
        'x_prompt': nrm(ks[0], (BATCH, SEQ, D), 1.0),
        'x_sample': nrm(ks[1], (DEC_BATCH, DEC_SEQ, D), 1.0),
        'state_hgrn': nrm(ks[2], (DEC_BATCH, DEPTH, 2, HG_HEADS, HG_DK, HG_DV), 0.5),
        'cache_gqa_k': nrm(ks[3], (DEC_BATCH, DEPTH, P, GQ_KV_HEADS, GQ_HD), 1.0),
        'cache_gqa_v': nrm(ks[4], (DEC_BATCH, DEPTH, P, GQ_KV_HEADS, GQ_HD), 1.0),
        'cache_na_k': nrm(ks[5], (DEC_BATCH, DEPTH, P, NA_HEADS, NA_HD), 1.0),
        'cache_na_v': nrm(ks[6], (DEC_BATCH, DEPTH, P, NA_HEADS, NA_HD), 1.0),
        'cache_mla_ckv': nrm(ks[7], (DEC_BATCH, DEPTH, P, ML_KV_RANK), 1.0),
        'cache_mla_krope': nrm(ks[8], (DEC_BATCH, DEPTH, P, ML_ROPE), 1.0),
        'c': nrm(ks[9], (DEC_BATCH, D), 1.0),
        'c_ctx': nrm(ks[10], (D,), 1.0),
        'w_ada': nrm(ks[11], (DEPTH, D, 6 * D), 0.5 * D ** -0.5),
        'b_ada': nrm(ks[12], (DEPTH, 6 * D), 0.01),
        'w_in': nrm(ks[13], (DEPTH, D, IN_WIDTH), D ** -0.5),
        'hg_lb_logits': nrm(ks[14], (DEPTH, 2, HG_HEADS * HG_DK), 1.0),
        'hg_gain': 1.0 + nrm(ks[15], (DEPTH, HG_DV), 0.05),
        'gq_q_gain': 1.0 + nrm(ks[16], (DEPTH, GQ_HD), 0.05),
        'gq_k_gain': 1.0 + nrm(ks[17], (DEPTH, GQ_HD), 0.05),
        'na_rpb': nrm(ks[18], (DEPTH, NA_HEADS, 2 * NA_WIN_R - 1, 2 * NA_WIN_C - 1), 0.5),
        'ml_q_a_gain': 1.0 + nrm(ks[19], (DEPTH, ML_Q_RANK), 0.05),
        'ml_kv_a_gain': 1.0 + nrm(ks[20], (DEPTH, ML_KV_RANK), 0.05),
        'ml_w_q_b': nrm(ks[21], (DEPTH, ML_Q_RANK, ML_HEADS * (ML_NOPE + ML_ROPE)), ML_Q_RANK ** -0.5),
        'ml_w_kv_b': nrm(ks[22], (DEPTH, ML_KV_RANK, ML_HEADS * (ML_NOPE + ML_V)), ML_KV_RANK ** -0.5),
        'w_branch': nrm(ks[23], (DEPTH, N_BRANCH, BRANCH_W, D), BRANCH_W ** -0.5),
        'w_out': nrm(ks[24], (DEPTH, D, D), D ** -0.5),
        'w_ffn_in': nrm(ks[25], (DEPTH, D, 2 * FFN_HIDDEN), D ** -0.5),
        'w_ffn_out': nrm(ks[26], (DEPTH, FFN_HIDDEN, D), FFN_HIDDEN ** -0.5),
        'final_gain': 1.0 + nrm(ks[27], (D,), 0.05),
    }


def reference(x_prompt, x_sample, state_hgrn, cache_gqa_k, cache_gqa_v, cache_na_k, cache_na_v,
              cache_mla_ckv, cache_mla_krope, c, c_ctx, w_ada, b_ada, w_in, hg_lb_logits, hg_gain,
              gq_q_gain, gq_k_gain, na_rpb, ml_q_a_gain, ml_kv_a_gain, ml_w_q_b, ml_w_kv_b,
              w_branch, w_out, w_ffn_in, w_ffn_out, final_gain):
    lb = jnp.cumsum(jax.nn.softmax(hg_lb_logits.astype(jnp.float32), axis=0), axis=0)
    lb = lb - lb[:1]
    y_p, y_s = x_prompt, x_sample
    per_layer = []
    for l in range(DEPTH):
        p = {
            'w_in': w_in[l], 'lb_f': lb[l, 0], 'lb_b': lb[l, 1], 'hg_gain': hg_gain[l],
            'gq_q_gain': gq_q_gain[l], 'gq_k_gain': gq_k_gain[l], 'na_rpb': na_rpb[l],
            'ml_q_a_gain': ml_q_a_gain[l], 'ml_kv_a_gain': ml_kv_a_gain[l],
            'ml_w_q_b': ml_w_q_b[l], 'ml_w_kv_b': ml_w_kv_b[l],
            'w_branch': w_branch[l], 'w_out': w_out[l],
            'w_ffn_in': w_ffn_in[l], 'w_ffn_out': w_ffn_out[l],
        }
        y_p, new = trunk_layer(y_p, adaln(c_ctx[None, :], w_ada[l], b_ada[l]), p)
        per_layer.append(new)
        ctx = {
            'hg': state_hgrn[:, l], 'gq_k': cache_gqa_k[:, l], 'gq_v': cache_gqa_v[:, l],
            'na_k': cache_na_k[:, l], 'na_v': cache_na_v[:, l],
            'ml_ckv': cache_mla_ckv[:, l], 'ml_kr': cache_mla_krope[:, l],
        }
        y_s, _ = trunk_layer(y_s, adaln(c, w_ada[l], b_ada[l]), p, ctx)
    y_prompt = rms_norm(y_p, final_gain)
    y_sample = rms_norm(y_s, final_gain)
    new_state_hgrn = jnp.stack([n[0] for n in per_layer], axis=1)
    new_cache_gqa_k = jnp.stack([n[1] for n in per_layer], axis=1)
    new_cache_gqa_v = jnp.stack([n[2] for n in per_layer], axis=1)
    new_cache_na_k = jnp.stack([n[3] for n in per_layer], axis=1)
    new_cache_na_v = jnp.stack([n[4] for n in per_layer], axis=1)
    new_cache_mla_ckv = jnp.stack([n[5] for n in per_layer], axis=1)
    new_cache_mla_krope = jnp.stack([n[6] for n in per_layer], axis=1)
    return (y_prompt, y_sample, new_state_hgrn, new_cache_gqa_k, new_cache_gqa_v,
            new_cache_na_k, new_cache_na_v, new_cache_mla_ckv, new_cache_mla_krope)
```

```python
import contextlib
import numpy as np
import ml_dtypes
import concourse.bass as bass
import concourse.mybir as mybir
from concourse.bass_utils import run_bass_kernel_spmd

F32 = mybir.dt.float32
BF16 = mybir.dt.bfloat16
AF = mybir.ActivationFunctionType
ALU = mybir.AluOpType
AX = mybir.AxisListType

D = 1024
T = 1024
L = 2
NTB = 8
EPS = 1e-6
IN_W = 9376
FFN = 2816
BIG = 30000.0
C_HQ, C_HFF, C_HFB, C_HI, C_HG = 0, 512, 1024, 1536, 2048
C_GQ, C_GK, C_GV = 2560, 3072, 3200
C_NQ, C_NK, C_NV = 3328, 3840, 4352
C_MQA, C_MKVA, C_MKR, C_GATE = 4864, 5120, 5248, 5280
GQ_SCALE = 64 ** -0.5
NA_SCALE = 64 ** -0.5
ML_SCALE = 96 ** -0.5
SERIAL = False
NSLOT = 3
SLOTW = 4096
ARENA = 39936


PSUMC = {"ps0", "ps1", "psS0", "psS1", "psO", "psSf"}


class Prog:
    def __init__(self, serial):
        self.ops = []
        self.lastw = {}
        self.readers = {}
        self.gcount = {}
        self.serial = serial
        self.cap = None

    def capture(self, f):
        self.cap = []
        f()
        c, self.cap = self.cap, None
        return c

    def replay(self, calls):
        for c in calls:
            self.op(*c)

    def replay_interleaved(self, a, b):
        na, nb = len(a), len(b)
        j = 0
        for i, c in enumerate(a):
            self.op(*c)
            while j < nb and (j + 1) * na <= (i + 1) * nb:
                self.op(*b[j])
                j += 1
        while j < nb:
            self.op(*b[j])
            j += 1

    def op(self, eng, fn, r=(), w=(), group=None):
        if self.cap is not None:
            self.cap.append((eng, fn, list(r), list(w), group))
            return -1
        i = len(self.ops)
        deps = set()
        gdeps = {}

        def add(j):
            o = self.ops[j]
            if o["group"] is not None:
                gdeps[o["group"]] = max(gdeps.get(o["group"], 0), self.gcount[o["group"]])
            else:
                deps.add(j)

        r = [x for c in r for x in ([("w", c[1], i) for i in range(4)] if (isinstance(c, tuple) and len(c) == 2 and c[0] == "w") else [c])]
        if group is not None and self.gcount.get(group, 0) > 0 and not self.serial:
            gdeps[group] = self.gcount[group]
        psc = [c for c in r if c in PSUMC]
        r = [c for c in r if c not in PSUMC] + ["phase"]
        w = list(w) + psc
        if self.serial:
            if i > 0:
                add(i - 1)
        else:
            for c in r:
                j = self.lastw.get(c)
                if j is not None:
                    add(j)
            for c in w:
                j = self.lastw.get(c)
                if j is not None:
                    add(j)
                for j in self.readers.get(c, ()):
                    add(j)
        for c in r:
            self.readers.setdefault(c, []).append(i)
        for c in w:
            self.lastw[c] = i
            self.readers[c] = []
        if group is not None:
            self.gcount[group] = self.gcount.get(group, 0) + 1
        self.ops.append(dict(eng=eng, fn=fn, deps=deps, gdeps=gdeps, group=group, signal=False, val=0))
        return i

    def fence(self):
        self.op("sp", lambda e: e.nop(), r=(), w=["phase"])

    def check(self):
        ops = self.ops
        engs = ["pe", "act", "dve", "pool", "sp"]
        q = {e: [i for i, o in enumerate(ops) if o["eng"] == e] for e in engs}
        ptr = {e: 0 for e in engs}
        done = [False] * len(ops)
        gdone = {}
        prog = True
        while prog:
            prog = False
            for e in engs:
                while ptr[e] < len(q[e]):
                    i = q[e][ptr[e]]
                    o = ops[i]
                    if all(done[j] for j in o["deps"]) and all(gdone.get(g, 0) >= c for g, c in o["gdeps"].items()):
                        done[i] = True
                        if o["group"] is not None:
                            gdone[o["group"]] = gdone.get(o["group"], 0) + 1
                        ptr[e] += 1
                        prog = True
                    else:
                        break
        stuck = {e: q[e][ptr[e]] for e in engs if ptr[e] < len(q[e])}
        for e, i in stuck.items():
            o = ops[i]
            print("STUCK", e, i, [j for j in o["deps"] if not done[j]], {g: (c, gdone.get(g, 0)) for g, c in o["gdeps"].items() if gdone.get(g, 0) < c})
        return not stuck

    def emit(self, nc, stack):
        assert self.check(), "deadlock in schedule"
        ops = self.ops
        for o in ops:
            for j in o["deps"]:
                if not (ops[j]["eng"] == "pe" and o["eng"] == "pe"):
                    ops[j]["signal"] = True
        engs = ["pe", "act", "dve", "pool", "sp"]
        cnt = {e: 0 for e in engs}
        gidx = {}
        for o in ops:
            if o["group"] is not None:
                gidx[o["group"]] = gidx.get(o["group"], 0) + 1
                o["val"] = 16 * gidx[o["group"]]
            elif o["signal"]:
                cnt[o["eng"]] += 1
                o["val"] = cnt[o["eng"]]
        esem = {e: stack.enter_context(nc.semaphore("es_" + e)) for e in engs}
        gsem = {g: stack.enter_context(nc.semaphore("gs_" + str(g))) for g in self.gcount}
        block = stack.enter_context(nc.Block())
        total_g = {g: 16 * n for g, n in self.gcount.items()}

        def run(engname, e):
            waited = {}
            for o in ops:
                if o["eng"] != engname:
                    continue
                need = {}
                for j in o["deps"]:
                    oj = ops[j]
                    if oj["eng"] == "pe" and engname == "pe":
                        continue
                    k = ("e", oj["eng"])
                    need[k] = max(need.get(k, 0), oj["val"])
                for g, c in o["gdeps"].items():
                    k = ("g", g)
                    need[k] = max(need.get(k, 0), 16 * c)
                for k, v in need.items():
                    if waited.get(k, 0) >= v:
                        continue
                    waited[k] = v
                    sem = esem[k[1]] if k[0] == "e" else gsem[k[1]]
                    e.wait_ge(sem, v)
                ins = o["fn"](e)
                if o["group"] is not None:
                    ins.then_inc(gsem[o["group"]], 16)
                elif o["signal"]:
                    ins.then_inc(esem[engname], 1)
            if engname == "sp":
                for g, v in total_g.items():
                    e.wait_ge(gsem[g], v)
                for en in engs:
                    if en != "sp" and cnt[en] > 0:
                        e.wait_ge(esem[en], cnt[en])

        @block.tensor
        def _(e):
            run("pe", e)

        @block.scalar
        def _(e):
            run("act", e)

        @block.vector
        def _(e):
            run("dve", e)

        @block.gpsimd
        def _(e):
            run("pool", e)

        @block.sync
        def _(e):
            run("sp", e)


def build(debug=False):
    nc = bass.Bass("TRN2", target_bir_lowering=False)
    P = Prog(SERIAL)
    stack = contextlib.ExitStack()
    din = {}
    dout = {}

    def DI(name, shape):
        din[name] = nc.dram_tensor(name, list(shape), F32, kind="ExternalInput").ap()
        return din[name]

    def DO(name, shape):
        dout[name] = nc.dram_tensor(name, list(shape), F32, kind="ExternalOutput").ap()
        return dout[name]

    x_d = DI("x", [T, D])
    cond_d = DI("cond", [1, D])
    w_ada_d = DI("w_ada", [L, D, 6 * D])
    b_ada_d = DI("b_ada", [L, 6 * D])
    w_in_d = DI("w_in", [L, D, IN_W])
    lbl_d = DI("hg_lb_logits", [L, 2, 512])
    hg_gain_d = DI("hg_gain", [L, 128])
    gqq_d = DI("gq_q_gain", [L, 64])
    gqk_d = DI("gq_k_gain", [L, 64])
    rpb_d = DI("na_rpb", [L, 8, 15, 31])
    mlq_g_d = DI("ml_q_a_gain", [L, 256])
    mlkv_g_d = DI("ml_kv_a_gain", [L, 128])
    wqb_d = DI("ml_w_q_b", [L, 256, 768])
    wkvb_d = DI("ml_w_kv_b", [L, 128, 1024])
    wbr_d = DI("w_branch", [L, 4, 512, D])
    wout_d = DI("w_out", [L, D, D])
    wfi_d = DI("w_ffn_in", [L, D, 2 * FFN])
    wfo_d = DI("w_ffn_out", [L, FFN, D])
    fg_d = DI("final_gain", [1, D])
    st0_d = DI("state0", [L, 2, 4, 128, 128])
    ckg_d = DI("ck_gqa", [L, 512, 128])
    cvg_d = DI("cv_gqa", [L, 512, 128])
    ckn_d = DI("ck_na", [L, 512, 512])
    cvn_d = DI("cv_na", [L, 512, 512])
    cckv_d = DI("c_ckv", [L, 512, 128])
    ckr_d = DI("c_kr", [L, 512, 32])
    ident_d = DI("ident", [128, 128])
    blk64_d = DI("blk64", [128, 128])
    pswap_d = DI("pswap", [128, 128])
    trif_d = DI("trif", [128, 256])
    trib_d = DI("trib", [128, 256])
    hmask_d = DI("hmask", [128, 512])
    def DIB(name, shape):
        din[name] = nc.dram_tensor(name, list(shape), BF16, kind="ExternalInput").ap()
        return din[name]

    kmask_d = DIB("kmask", [32, 1536])
    qmf_d = DIB("qmask_full", [32, T])
    qmn_d = DIB("qmask_na", [32, T])
    colm_d = DI("colmask", [128, 64])
    cosg_d = DI("cos_g", [128, T])
    sing_d = DI("sin_g", [128, T])
    csm_d = DI("cossin_m", [128, T])
    keep_d = DI("keep", [128, 1])

    y_d = DO("y", [T, D])
    ost_d = DO("o_state", [4, L, 2, 4, 128, 128])
    ogk_d = DO("o_gqk", [4, L, 256, 128])
    ogv_d = DO("o_gqv", [4, L, 256, 128])
    onk_d = DO("o_nak", [4, L, 256, 512])
    onv_d = DO("o_nav", [4, L, 256, 512])
    ockv_d = DO("o_ckv", [4, L, 256, 128])
    okr_d = DO("o_kr", [4, L, 256, 32])
    bscr = nc.dram_tensor("bscr", [L * 120, 64 * 128], F32, kind="Internal").ap()

    def SB(name, shape, dt):
        return stack.enter_context(nc.sbuf_tensor("sb_" + name, list(shape), dt))

    def PS(name, shape):
        return stack.enter_context(nc.psum_tensor("pt_" + name, list(shape), F32))

    xT = SB("xT", [128, 8, T], F32)
    hT = SB("hT", [128, 8, T], BF16)
    wsl = [SB("wsl%d" % i, [128, SLOTW], BF16) for i in range(NSLOT)]
    OB = SB("OB", [128, 4, 4, T], BF16)
    arena = SB("arena", [128, ARENA], BF16)
    ident = SB("identf", [128, 128], F32)
    identb = SB("identb", [128, 128], BF16)
    onesb = SB("onesb", [128, 128], BF16)
    blk64 = SB("blk64b", [128, 128], BF16)
    pswap = SB("pswapb", [128, 128], BF16)
    trif = SB("triff", [128, 256], F32)
    trib = SB("tribf", [128, 256], F32)
    hmask = SB("hmask", [128, 512], F32)
    cosg = SB("cosg", [128, T], BF16)
    sing = SB("sing", [128, T], BF16)
    csm = SB("csm", [128, T], BF16)
    colm = SB("colm", [128, 64], F32)
    keep = SB("keep", [128, 1], F32)
    modT = SB("modT", [128, L, 48], F32)
    opsc = SB("opsc", [128, L, 2, 8], F32)
    badaT = SB("badaT", [128, L, 48], F32)
    condT = SB("condT", [128, 8], F32)
    condb = SB("condb", [128, 8], BF16)
    gcols = SB("gcols", [128, 16], F32)
    lbt = SB("lbt", [128, 2, 512], F32)

    rstd = SB("rstd", [128, T], F32)
    gbc = SB("gbc", [128, 512], F32)
    smallf = SB("smallf", [128, 256], F32)

    ps0 = PS("ps0", [128, 1024])
    ps1 = PS("ps1", [128, 1024])
    psSf = PS("psS", [128, 1024])
    psS = [psSf[:, 0:512], psSf[:, 512:1024]]
    psO = PS("psO", [128, 1024])

    def af32(off, n):
        return arena[:, off:off + n].bitcast(F32)

    def abf(off, n):
        return arena[:, off:off + n]

    cnt_q = [0]

    def dma(eng, out, in_, r=(), w=(), group=None, slow=False):
        if group is None:
            cnt_q[0] += 1
            group = "d%s%d" % (eng, cnt_q[0] % 8)
        elif group == "out":
            cnt_q[0] += 1
            group = "out_%s%d" % (eng, cnt_q[0] % 8)
        kw = dict(allow_slow_non_contiguous=True) if slow else {}
        return P.op(eng, lambda e: e.dma_start(out=out, in_=in_, **kw), r=r, w=w, group=group)

    def act(out, in_, func, r=(), w=(), bias=None, scale=None):
        kw = {}
        if bias is not None:
            kw["bias"] = bias
        if scale is not None:
            kw["scale"] = scale
        return P.op("act", lambda e: e.activation(out=out, in_=in_, func=func, **kw), r=r, w=w)

    def tt(eng, out, in0, in1, op, r=(), w=()):
        return P.op(eng, lambda e: e.tensor_tensor(out=out, in0=in0, in1=in1, op=op), r=r, w=w)

    def ts(eng, out, in0, s1, s2, op0, op1=None, r=(), w=()):
        if op1 is None:
            return P.op(eng, lambda e: e.tensor_scalar(out=out, in0=in0, scalar1=s1, scalar2=None, op0=op0), r=r, w=w)
        return P.op(eng, lambda e: e.tensor_scalar(out=out, in0=in0, scalar1=s1, scalar2=s2, op0=op0, op1=op1), r=r, w=w)

    def stt(eng, out, in0, scalar, in1, op0, op1, r=(), w=()):
        eng = "dve"
        return P.op(eng, lambda e: e.scalar_tensor_tensor(out=out, in0=in0, scalar=scalar, in1=in1, op0=op0, op1=op1), r=r, w=w)

    def cp(eng, out, in_, r=(), w=()):
        if eng == "act":
            return P.op(eng, lambda e: e.activation(out=out, in_=in_, func=AF.Copy), r=r, w=w)
        return P.op(eng, lambda e: e.tensor_copy(out=out, in_=in_), r=r, w=w)

    def mm(out, lhsT, rhs, start, stop, r=(), w=()):
        return P.op("pe", lambda e: e.matmul(out, lhsT, rhs, start=start, stop=stop, skip_group_check=True), r=r, w=w)

    def tp(out, in_, idt, r=(), w=()):
        return P.op("pe", lambda e: e.transpose(out, in_, idt), r=r, w=w)

    def memset(eng, ap, v, w=()):
        return P.op(eng, lambda e: e.memset(ap, v), w=w)

    def rstd_from(ps_ap, n, out_ap, pc, oc):
        act(out_ap, ps_ap, AF.Ln, r=[pc], w=[oc], bias=epsc[:, 0:1], scale=1.0 / n)
        act(out_ap, out_ap, AF.Exp, r=[oc], w=[oc], scale=-0.5)

    epsc = SB("epsc", [128, 1], F32)
    memset("dve", epsc[:], EPS, w=["epsc"])
    onec = SB("onec", [128, 1], F32)
    memset("dve", onec[:], 1.0, w=["onec"])
    memset("dve", onesb[:], 1.0, w=["onesb"])

    def load_const(dst, src, name, cast_to=None):
        if cast_to is None:
            dma("sp", dst[:], src, w=[name])
        else:
            dma("pool", cast_to[:], src, w=[name])

    load_const(ident, ident_d[:, :], "ident")
    load_const(None, ident_d[:, :], "identb", cast_to=identb)
    load_const(None, blk64_d[:, :], "blk64", cast_to=blk64)
    load_const(None, pswap_d[:, :], "pswap", cast_to=pswap)
    load_const(trif, trif_d[:, :], "trif")
    load_const(trib, trib_d[:, :], "trib")
    load_const(hmask, hmask_d[:, :], "hmask")
    load_const(None, cosg_d[:, :], "cosg", cast_to=cosg)
    load_const(None, sing_d[:, :], "sing", cast_to=sing)
    load_const(None, csm_d[:, :], "csm", cast_to=csm)
    load_const(colm, colm_d[:, :], "colm")
    load_const(keep, keep_d[:, :], "keep")
    dma("sp", condT[:], cond_d[0, :].rearrange("(c p) -> p c", p=128), w=["condT"], slow=True)
    for l in range(L):
        dma("sp", badaT[:, l, :], b_ada_d[l, :].rearrange("(c p) -> p c", p=128), w=["badaT"], slow=True)
    fgc = SB("fgc", [128, 8], F32)
    dma("sp", fgc[:], fg_d[0, :].rearrange("(c p) -> p c", p=128), w=["fgc"], slow=True)
    for l in range(L):
        dma("sp", gcols[:, l:l + 1], hg_gain_d[l, :].rearrange("(p o) -> p o", o=1), w=["gcols"], slow=True)
        for hh in range(2):
            dma("sp", gcols[hh * 64:(hh + 1) * 64, 2 + l:3 + l], gqq_d[l, :].rearrange("(p o) -> p o", o=1), w=["gcols"], slow=True)
            dma("sp", gcols[hh * 64:(hh + 1) * 64, 4 + l:5 + l], gqk_d[l, :].rearrange("(p o) -> p o", o=1), w=["gcols"], slow=True)
        dma("sp", gcols[:, 6 + 2 * l:8 + 2 * l], mlq_g_d[l, :].rearrange("(c p) -> p c", p=128), w=["gcols"], slow=True)
        dma("sp", gcols[:, 10 + l:11 + l], mlkv_g_d[l, :].rearrange("(p o) -> p o", o=1), w=["gcols"], slow=True)
    for dr in range(2):
        dma("sp", lbt[:, dr, :], lbl_d[1, dr:dr + 1, :].partition_broadcast(128), w=["lbt"])
        dma("sp", rstd[:, dr * 512:(dr + 1) * 512], lbl_d[0, dr:dr + 1, :].partition_broadcast(128), w=["rstd"])
    tt("dve", lbt[:, :, :], lbt[:, :, :], rstd[:, :].rearrange("p (d n) -> p d n", n=512), ALU.subtract, r=["lbt", "rstd"], w=["lbt"])
    act(lbt[:, :, :], lbt[:, :, :], AF.Sigmoid, r=["lbt"], w=["lbt"])

    jobs = []
    jobnames = []

    def job(load, compute):
        jobs.append((load, compute))
        jobnames.append(getattr(compute, "__name__", "?"))
        if auto_defer[0] and getattr(compute, "__name__", "") != "<lambda>":
            drain(1)

    def wload(slot, views):
        assert len(views) <= 4
        for i, (dst, src) in enumerate(views):
            dma("pool", dst, src, w=[("w", slot, i)], group="w%d_%d" % (slot, i))
        for i in range(len(views), 4):
            pass

    def sl3(slot, nk, ncol, off=0):
        return wsl[slot][:, off:off + nk * ncol].rearrange("p (k n) -> p k n", n=ncol)

    def wsrc(w2d, c0, ncol):
        return w2d[:, c0:c0 + ncol].rearrange("(k p) n -> p k n", p=128)

    xtok = af32(0, 2 * 8 * T).rearrange("p (n d) -> p n d", d=D)
    for n in range(8):
        dma("sp" if n % 2 == 0 else "act", xtok[:, n, :], x_d[n * 128:(n + 1) * 128, :], w=[("xtok", n)])
    for fc in range(8):
        pst = ps0 if fc % 2 == 0 else ps1
        pc = "ps0" if fc % 2 == 0 else "ps1"
        for tb in range(8):
            tp(pst[:, tb * 128:(tb + 1) * 128], xtok[:, tb, fc * 128:(fc + 1) * 128], ident[:], r=[("xtok", tb), "ident"], w=[pc])
        cp("act" if fc % 2 == 0 else "dve", xT[:, fc, :], pst[:], r=[pc], w=[("xT", fc)])
    P.fence()

    act(condT[:], condT[:], AF.Silu, r=["condT"], w=["condT"])
    cp("dve", condb[:], condT[:], r=["condT"], w=["condb"])
    deferred = []
    auto_defer = [False]

    def ada_job(l, g):
        def ld(slot):
            wload(slot, [(sl3(slot, 8, 512), wsrc(w_ada_d[l], g * 512, 512))])

        def comp(slot):
            w3 = sl3(slot, 8, 512)
            pst, pc = nextps()
            for m in range(4):
                for k in range(8):
                    mm(pst[:, m:m + 1], w3[:, k, m * 128:(m + 1) * 128], condb[:, k:k + 1], k == 0, k == 7,
                       r=[("w", slot), "condb"], w=[pc])
            j0 = 4 * g
            tt("dve", modT[:, l, j0:j0 + 4], pst[:, 0:4], badaT[:, l, j0:j0 + 4], ALU.add, r=[pc, "badaT"], w=["modT"])
            if 8 <= j0 < 16:
                ts("dve", opsc[:, l, 0, j0 - 8:j0 - 4], modT[:, l, j0:j0 + 4], 1.0, None, ALU.add, r=["modT"], w=["opsc"])
            if 32 <= j0 < 40:
                ts("dve", opsc[:, l, 1, j0 - 32:j0 - 28], modT[:, l, j0:j0 + 4], 1.0, None, ALU.add, r=["modT"], w=["opsc"])
        return ld, comp

    for g in range(4):
        jobs.append(ada_job(0, g))
        jobnames.append("ada")
    for g in range(4, 12):
        deferred.append(ada_job(0, g))
    for g in range(12):
        deferred.append(ada_job(1, g))

    def drain(n):
        while n > 0 and deferred:
            jobs.append(deferred.pop(0))
            jobnames.append("ada_d")
            n -= 1

    def norm_mod(l, which):
        sh0 = 0 if which == 0 else 24

        def comp(slot):
            for c in range(8):
                if c % 2 == 0:
                    act(hT[:, c, :], xT[:, c, :], AF.Square, r=[("xT", c)], w=[("hT", c)])
                else:
                    tt("dve", hT[:, c, :], xT[:, c, :], xT[:, c, :], ALU.mult, r=[("xT", c)], w=[("hT", c)])
            for th in range(2):
                for c in range(8):
                    mm(ps0[:, th * 512:(th + 1) * 512], onesb[:], hT[:, c, th * 512:(th + 1) * 512], c == 0, c == 7,
                       r=[("hT", c), "onesb"], w=["ps0"])
            rstd_from(ps0[:], float(D), rstd[:], "ps0", "rstd")
            tmp = af32(ARENA - 2 * T * 2, 2 * T * 2).rearrange("p (b t) -> p b t", t=T)
            for c in range(8):
                tb_ = tmp[:, c % 2, :]
                stt("dve" if c % 2 == 0 else "pool", tb_, xT[:, c, :], opsc[:, l, which, c:c + 1], rstd[:], ALU.mult, ALU.mult,
                    r=[("xT", c), "opsc", "rstd"], w=[("nmt", c % 2)])
                act(hT[:, c, :], tb_, AF.Identity, r=[("nmt", c % 2), "modT"], w=[("hT", c)],
                    bias=modT[:, l, sh0 + c:sh0 + c + 1], scale=1.0)
        job(None, comp)

    def proj_fm(slot, w3, ncolchunk, m0, M, pst, pc, rhs_tile, rcell, nk):
        for th in range(2):
            for k in range(nk):
                mm(pst[0:M, th * 512:(th + 1) * 512], w3[:, k, m0:m0 + M], rhs_tile[:, k, th * 512:(th + 1) * 512], k == 0, k == nk - 1,
                   r=[("w", slot)] + [(rcell, k)], w=[pc])

    def proj_tm(slot, w3, n0, N, tb, pst, pc):
        for k in range(8):
            mm(pst[:, 0:N], hT[:, k, tb * 128:(tb + 1) * 128], w3[:, k, n0:n0 + N], k == 0, k == 7,
               r=[("w", slot), ("hT", k)], w=[pc])

    pp = [0]
    wide = [False]

    def nextps():
        if wide[0]:
            pp[0] = (pp[0] + 1) % 4
            return [(ps0, "ps0"), (ps1, "ps1"), (psSf, "psSf"), (psO, "psO")][pp[0]]
        pp[0] = (pp[0] + 1) % 2
        return (ps0, "ps0") if pp[0] else (ps1, "ps1")

    def set_wide(v):
        job(None, lambda slot: wide.__setitem__(0, v))

    def seg_out(dst_d, l, tb, width, src_ap, rc):
        seg, half = tb // 2, tb % 2
        dma("sp", dst_d[seg, l, half * 128:(half + 1) * 128, :], src_ap, r=[rc], group="out")

    H_QT = 0
    H_GT = H_QT + 4 * T
    H_VT = H_GT + 4 * T
    H_BT = H_VT + 8 * 512
    H_LF = H_BT + 4 * T
    H_E = H_LF + 4 * T
    H_QD = H_E + 4 * T
    H_KD = H_QD + 2 * T
    H_KDT = H_KD + 2 * T
    H_AT = H_KDT + 2 * T
    H_S = H_AT + 256
    H_SP = H_S + 512
    H_TMP = H_SP + 256
    H_LFT = H_TMP + 512
    H_COL = H_LFT + 512
    H_QIN = H_COL + 2 * 5 * 32
    H_KEND = H_QIN + 2 * T
    H_ZB = H_KEND + 2 * T
    H_E2 = H_ZB + 128
    H_END = H_E2 + 2 * T

    def hgrn(l):
        qT = abf(H_QT, 4 * T).rearrange("p (h t) -> p h t", t=T)
        gT = abf(H_GT, 4 * T).rearrange("p (h t) -> p h t", t=T)
        vtok = abf(H_VT, 8 * 512).rearrange("p (b n) -> p b n", n=512)
        bT = af32(H_BT, 4 * T).rearrange("p (d t) -> p d t", t=T)
        lfT = af32(H_LF, 4 * T).rearrange("p (d t) -> p d t", t=T)
        eT = af32(H_E, 4 * T).rearrange("p (d t) -> p d t", t=T)
        qd = abf(H_QD, 2 * T).rearrange("p (d t) -> p d t", t=T)
        kd = abf(H_KD, 2 * T).rearrange("p (d t) -> p d t", t=T)
        kdt = abf(H_KDT, 2 * T).rearrange("p (d b n) -> p d b n", d=2, n=128)
        AT = abf(H_AT, 256).rearrange("p (d n) -> p d n", n=128)
        S = af32(H_S, 512).rearrange("p (d n) -> p d n", n=128)
        Sp = abf(H_SP, 256).rearrange("p (d n) -> p d n", n=128)
        tmpS = af32(H_TMP, 512)
        lfts = [af32(H_LFT, 512), af32(H_TMP, 512)]
        cols = af32(H_COL, 2 * 5 * 32).rearrange("p (a d c) -> p a d c", a=5, d=2)
        qin = abf(H_QIN, 2 * T).rearrange("p (d t) -> p d t", t=T)
        kend = abf(H_KEND, 2 * T).rearrange("p (d t) -> p d t", t=T)
        zb = abf(H_ZB, 128)
        eT2 = af32(H_E2, 2 * T)

        def ld_q(slot):
            wload(slot, [(sl3(slot, 8, 512), wsrc(w_in_d[l], C_HQ, 512))])

        def c_q(slot):
            w3 = sl3(slot, 8, 512)
            for h in range(4):
                pst, pc = nextps()
                proj_fm(slot, w3, 512, h * 128, 128, pst, pc, hT, "hT", 8)
                act(qT[:, h, :], pst[:], AF.Copy, r=[pc], w=[("hqT", h)], scale=128 ** -0.5)
        job(ld_q, c_q)

        def ld_g(slot):
            wload(slot, [(sl3(slot, 8, 512), wsrc(w_in_d[l], C_HG, 512))])

        def c_g(slot):
            w3 = sl3(slot, 8, 512)
            for h in range(4):
                pst, pc = nextps()
                proj_fm(slot, w3, 512, h * 128, 128, pst, pc, hT, "hT", 8)
                act(gT[:, h, :], pst[:], AF.Silu, r=[pc], w=[("hgT", h)])
        job(ld_g, c_g)

        def ld_v(slot):
            wload(slot, [(sl3(slot, 8, 512), wsrc(w_in_d[l], C_HI, 512))])

        def c_v(slot):
            w3 = sl3(slot, 8, 512)
            for tb in range(8):
                pst, pc = nextps()
                proj_tm(slot, w3, 0, 512, tb, pst, pc)
                cp("act" if tb % 2 else "dve", vtok[:, tb, :], pst[:, 0:512], r=[pc], w=[("hvt", tb)])
        job(ld_v, c_v)

        for h in range(4):
            def ld_f(slot, h=h):
                w3 = sl3(slot, 8, 256)
                wload(slot, [(w3[:, :, 0:128], wsrc(w_in_d[l], C_HFF + h * 128, 128)),
                             (w3[:, :, 128:256], wsrc(w_in_d[l], C_HFB + h * 128, 128))])

            def c_f(slot, h=h):
                import os
                ksub = int(os.environ.get("KSUB", "99"))
                w3 = sl3(slot, 8, 256)
                neg = (l == 0)

                def s1_proj(tb):
                    pst, pc = nextps()
                    proj_tm(slot, w3, 0, 256, tb, pst, pc)
                    lft_ = lfts[tb % 2]
                    lfc = ("lft", tb % 2)
                    act(lft_[:], pst[:, 0:256], AF.Exp, r=[pc], w=[lfc], scale=-1.0)
                    if l == 0:
                        act(lft_[:], lft_[:], AF.Ln, r=[lfc], w=[lfc], bias=onec[:, 0:1], scale=1.0)
                    else:
                        l3 = lft_[:].rearrange("p (d n) -> p d n", n=128)
                        t3 = smallf[:, 0:256].rearrange("p (d n) -> p d n", n=128)
                        ts("dve", lft_[:], lft_[:], 1.0, None, ALU.add, r=[lfc], w=[lfc])
                        P.op("dve", lambda e, lft_=lft_: e.reciprocal(out=lft_[:], in_=lft_[:]), r=[lfc], w=[lfc])
                        ts("dve", t3, l3, -1.0, 1.0, ALU.mult, ALU.add, r=[lfc], w=["smallf"])
                        tt("dve", t3, t3, lbt[:, :, h * 128:(h + 1) * 128], ALU.mult, r=["smallf", "lbt"], w=["smallf"])
                        tt("dve", l3, l3, t3, ALU.add, r=[lfc, "smallf"], w=[lfc])
                        act(lft_[:], lft_[:], AF.Ln, r=[lfc], w=[lfc])

                def s1_cum(tb):
                    lft_ = lfts[tb % 2]
                    lfc = ("lft", tb % 2)
                    for dr in range(2):
                        ps_ = psS[dr]
                        pcs = "psS%d" % dr
                        mm(ps_[:, 0:256], lft_[:, dr * 128:(dr + 1) * 128], (trif if dr == 0 else trib)[:], True, True,
                           r=[lfc, "trif", "trib"], w=[pcs])
                        if neg:
                            ts("dve", bT[:, dr, tb * 128:(tb + 1) * 128], ps_[:, 0:128], -1.0, None, ALU.mult, r=[pcs], w=["bT"])
                            act(lfT[:, dr, tb * 128:(tb + 1) * 128], ps_[:, 128:256], AF.Copy, r=[pcs], w=["lfT"], scale=-1.0)
                        else:
                            cp("dve", bT[:, dr, tb * 128:(tb + 1) * 128], ps_[:, 0:128], r=[pcs], w=["bT"])
                            cp("act", lfT[:, dr, tb * 128:(tb + 1) * 128], ps_[:, 128:256], r=[pcs], w=["lfT"])

                s1_proj(0)
                for tb in range(8):
                    if tb + 1 < 8:
                        s1_proj(tb + 1)
                    s1_cum(tb)
                if ksub <= 1:
                    return
                memset("pool", zb, 0.0, w=["zb"])
                for th in range(2):
                    mm(psO[:, th * 512:(th + 1) * 512], zb, hT[:, 0, th * 512:(th + 1) * 512], True, True, r=["zb", ("hT", 0)], w=["psO"])
                obs = OB[:, 1, :, :]
                dbufs = [(abf(H_QD, T), abf(H_QD + T, T), abf(H_KD, T), abf(H_KD + T, T)),
                         (obs[:, 0, :], obs[:, 1, :], obs[:, 2, :], obs[:, 3, :])]

                def elem_part(dr):
                    qdH, qdX, kdH, kdX = dbufs[dr]
                    cqdH, cqdX, ckdH, ckdX = [(n_, dr) for n_ in ("qdH", "qdX", "kdH", "kdX")]
                    b3 = bT[:, dr, :].rearrange("p (c t) -> p c t", t=64)
                    b4 = bT[:, dr, :].rearrange("p (c t) -> p c t", t=32)
                    rbcol = cols[:, 0, dr, 0:16]
                    blast = cols[:, 1, dr, 0:16]
                    cp("dve", rbcol, b3[:, :, 31 if dr == 0 else 32], r=["bT"], w=["hcols"])
                    cp("dve", blast, b3[:, :, 63 if dr == 0 else 0], r=["bT"], w=["hcols"])
                    act(cols[:, 4, dr, 0:16], blast, AF.Exp, r=["hcols"], w=["hcols"])
                    e23 = eT2.rearrange("p (c t) -> p c t", t=64)
                    e13 = eT[:, 1, :].rearrange("p (c t) -> p c t", t=64)
                    blb = blast.rearrange("p (c o) -> p c o", o=1).to_broadcast([128, 16, 64])
                    rbb = rbcol.rearrange("p (c o) -> p c o", o=1).to_broadcast([128, 16, 64])
                    rh = smallf[:, 0:32]
                    rhb = rh.rearrange("p (c o) -> p c o", o=1).to_broadcast([128, 32, 32])
                    kT_ = lfT[:, dr, :]
                    act(kT_, kT_, AF.Exp, r=["lfT"], w=["lfT"])
                    tt("dve", e23, b3, blb, ALU.subtract, r=["bT", "hcols"], w=["eT2"])
                    tt("dve", e13, b3, rbb, ALU.subtract, r=["bT", "hcols"], w=[("eT", 1)])
                    ts("pool", kT_, kT_, -1.0, 1.0, ALU.mult, ALU.add, r=["lfT"], w=["lfT"])
                    act(eT2, eT2, AF.Exp, r=["eT2"], w=["eT2"], scale=-1.0)
                    ts("dve", eT[:, 0, :], eT[:, 1, :], 0.0, None, ALU.min, r=[("eT", 1)], w=[("eT", 0)])
                    ts("dve", eT[:, 1, :], eT[:, 1, :], 0.0, None, ALU.max, r=[("eT", 1)], w=[("eT", 1)])
                    act(eT[:, 0, :], eT[:, 0, :], AF.Exp, r=[("eT", 0)], w=[("eT", 0)])
                    act(eT[:, 1, :], eT[:, 1, :], AF.Exp, r=[("eT", 1)], w=[("eT", 1)], scale=-1.0)
                    tt("dve", kend[:, dr, :], kT_, eT2, ALU.mult, r=["lfT", "eT2"], w=[("kend", dr)])
                    tt("dve", qdX, qT[:, h, :], eT[:, 0, :], ALU.mult, r=[("hqT", h), ("eT", 0)], w=[cqdX])
                    tt("dve", kdX, kT_, eT[:, 1, :], ALU.mult, r=["lfT", ("eT", 1)], w=[ckdX])
                    act(eT2, bT[:, dr, :], AF.Exp, r=["bT"], w=["eT2"])
                    cp("dve", rh, b4[:, :, 16], r=["bT"], w=["smallf"])
                    tt("dve", b4, b4, rhb, ALU.subtract, r=["bT", "smallf"], w=["bT"])
                    ts("dve", bT[:, dr, :], bT[:, dr, :], 41.0, -41.0, ALU.min, ALU.max, r=["bT"], w=["bT"])
                    act(eT[:, 0, :], bT[:, dr, :], AF.Exp, r=["bT"], w=[("eT", 0)])
                    act(eT[:, 1, :], bT[:, dr, :], AF.Exp, r=["bT"], w=[("eT", 1)], scale=-1.0)
                    tt("dve", qin[:, dr, :], qT[:, h, :], eT2, ALU.mult, r=[("hqT", h), "eT2"], w=[("qin", dr)])
                    tt("dve", qdH, qT[:, h, :], eT[:, 0, :], ALU.mult, r=[("hqT", h), ("eT", 0)], w=[cqdH])
                    tt("dve", kdH, kT_, eT[:, 1, :], ALU.mult, r=["lfT", ("eT", 1)], w=[ckdH])

                def intra_part(dr):
                    qdH, qdX, kdH, kdX = dbufs[dr]
                    cqdH, cqdX, ckdH, ckdX = [(n_, dr) for n_ in ("qdH", "qdX", "kdH", "kdX")]
                    for tb in range(8):
                        pb = psS[tb % 2][:].bitcast(BF16)
                        pcs = "psS%d" % (tb % 2)
                        tp(pb[:, 0:128], kend[:, dr, tb * 128:(tb + 1) * 128], identb[:], r=[("kend", dr), "identb"], w=[pcs])
                        cp("act" if tb % 2 else "dve", kdt[:, dr, tb, :], pb[:, 0:128], r=[pcs], w=[("kdt", dr)])
                    for tb in range(8):
                        blk = slice(tb * 128, (tb + 1) * 128)
                        for x_, (kk, qq, kc_, qc_) in enumerate([(kdH, qdH, ckdH, cqdH), (kdX, qdX, ckdX, cqdX)]):
                            ps_ = psS[x_]
                            pcs = "psS%d" % x_
                            mo = (2 * x_ + dr) * 128
                            mm(ps_[:, 0:128], kk[:, blk], qq[:, blk], True, True, r=[kc_, qc_], w=[pcs])
                            tt("dve", AT[:, x_, :], ps_[:, 0:128], hmask[:, mo:mo + 128], ALU.mult, r=[pcs, "hmask"], w=[("AT", x_)])
                            mm(psO[:, blk], vtok[:, tb, h * 128:(h + 1) * 128], AT[:, x_, :], False, False,
                               r=[("hvt", tb), ("AT", x_)], w=["psO"])

                P.replay(P.capture(lambda: elem_part(0)))
                P.replay_interleaved(P.capture(lambda: intra_part(0)), P.capture(lambda: elem_part(1)))
                P.replay(P.capture(lambda: intra_part(1)))
                if ksub <= 4:
                    return
                for dr in range(2):
                    dma("sp", S[:, dr, :], st0_d[l, dr, h, :, :], w=[("S", dr)])
                ubuf = [[(psS[0], "psS0"), (ps0, "ps0")], [(psS[1], "psS1"), (ps1, "ps1")]]

                def emit_U(i, dr):
                    c = i if dr == 0 else 15 - i
                    tb, half = c // 2, c % 2
                    pu, puc = ubuf[dr][i % 2]
                    mm(pu[:, 0:128], kdt[half * 64:(half + 1) * 64, dr, tb, :], vtok[half * 64:(half + 1) * 64, tb, h * 128:(h + 1) * 128],
                       True, True, r=[("kdt", dr), ("hvt", tb)], w=[puc])

                for dr in range(2):
                    emit_U(0, dr)
                for i in range(16):
                    if i + 1 < 16:
                        for dr in range(2):
                            emit_U(i + 1, dr)
                    for dr in range(2):
                        c = i if dr == 0 else 15 - i
                        segstart = (c % 4 == 0 and c > 0) if dr == 0 else (c % 4 == 3 and c < 15)
                        segend = (c % 4 == 3) if dr == 0 else (c % 4 == 0)
                        if segstart:
                            ts("dve", S[:, dr, :], S[:, dr, :], keep[:, 0:1], None, ALU.mult, r=[("S", dr), "keep"], w=[("S", dr)])
                        act(Sp[:, dr, :], S[:, dr, :], AF.Copy, r=[("S", dr)], w=[("Sp", dr)])
                        mm(psO[:, c * 64:(c + 1) * 64], Sp[:, dr, :], qin[:, dr, c * 64:(c + 1) * 64], False, (i == 15 and dr == 1),
                           r=[("Sp", dr), ("qin", dr)], w=["psO"])
                        pu, puc = ubuf[dr][i % 2]
                        stt("dve", S[:, dr, :], S[:, dr, :], cols[:, 4, dr, c:c + 1], pu[:, 0:128], ALU.mult, ALU.add,
                            r=[("S", dr), puc, "hcols"], w=[("S", dr)])
                        if segend:
                            dma("sp", ost_d[c // 4, l, dr, h, :, :], S[:, dr, :], r=[("S", dr)], group="out")
                if ksub <= 5:
                    return
                on = af32(H_E, 2 * T)
                sq = abf(H_KD, T)
                act(sq, psO[:], AF.Square, r=["psO", ("kdH", 0)], w=[("kdH", 0)])
                for th in range(2):
                    mm(ps0[:, th * 512:(th + 1) * 512], onesb[:], sq[:, th * 512:(th + 1) * 512], True, True, r=[("kdH", 0), "onesb"], w=["ps0"])
                if ksub <= 8:
                    return
                rstd_from(ps0[:], 128.0, rstd[:], "ps0", "rstd")
                if ksub <= 9:
                    return
                tt("dve", on, psO[:], rstd[:], ALU.mult, r=["psO", "rstd", ("eT", 0)], w=[("eT", 0)])
                if ksub <= 10:
                    return
                ts("dve", on, on, gcols[:, l:l + 1], None, ALU.mult, r=[("eT", 0), "gcols"], w=[("eT", 0)])
                tt("dve", OB[:, 0, h, :], on, gT[:, h, :], ALU.mult, r=[("eT", 0), ("hgT", h)], w=[("OB", 0, h)])
            job(ld_f, c_f)

    A_Q = 0
    A_K = A_Q + 8 * T
    A_V = A_K + 8 * 1536
    A_V1 = A_V + 12 * 512
    A_PT = A_V1 + 2 * 12 * 128
    A_PSI = A_PT + 1024
    A_STG = A_PSI + 2048
    A_TMP = A_STG + 3072
    A_END = A_TMP + 4096
    assert A_END <= ARENA, A_END
    assert H_END <= ARENA, H_END

    def aviews():
        Q = abf(A_Q, 8 * T).rearrange("p (h t) -> p h t", t=T)
        K = abf(A_K, 8 * 1536).rearrange("p (h t) -> p h t", t=1536)
        V = abf(A_V, 12 * 512).rearrange("p (b h d) -> p b h d", h=8, d=64)
        V1 = abf(A_V1, 2 * 12 * 128).rearrange("p (s b n) -> p s b n", s=2, n=128)
        PT = abf(A_PT, 1024).rearrange("p (s n) -> p s n", n=512)
        PSI = abf(A_PSI, 1920).rearrange("p (s n) -> p s n", n=1920)
        STG = af32(A_STG, 3072).rearrange("p (b n) -> p b n", n=512)
        TMP = af32(A_TMP, 4096).rearrange("p (b n) -> p b n", n=T)
        return Q, K, V, V1, PT, PSI, STG, TMP

    def attn_masks(qm_d, nheads_k):
        Q, K, V, V1, PT, PSI, STG, TMP = aviews()
        dma("sp", Q[64:96, :, :], qm_d[:, :].rearrange("p (o t) -> p o t", o=1).to_broadcast([32, 8, T]), w=[("Q", h) for h in range(8)])
        dma("act", K[64:96, 0:nheads_k, :], kmask_d[:, :].rearrange("p (o t) -> p o t", o=1).to_broadcast([32, nheads_k, 1536]),
            w=[("K", h) for h in range(nheads_k)])
        memset("pool", V1[:, :, :, 64:128], 1.0, w=[("V1", 0), ("V1", 1)])

    def attn_core(branch, kvmap, krows, scale, mask_base, psi_fn=None):
        Q, K, V, V1, PT, PSI, STG, TMP = aviews()
        recip = TMP[0:64, 1, :]
        def kbs(qt):
            if psi_fn is None:
                return list(range(12))
            lo, hi = (0, 5) if qt == 0 else (2, 7)
            return [0, 1, 2, 3] + [4 + k for k in range(lo, hi + 1)]
        steps = [(h, qt, kb) for h in range(8) for qt in range(2) for kb in kbs(qt)]
        accs = [(psO, "psO"), (ps0, "ps0")]
        sbufs = [(psS[0], "psS0"), (psS[1], "psS1"), (ps1[:, 0:512], "ps1")]

        def emit_S(i):
            h, qt, kb = steps[i]
            kv = kvmap(h)
            if qt == 0 and kb == 0:
                vb = h % 2
                cp("pool", V1[:, vb, :, 0:64], V[:, :, kv, :], r=[("V", kv)], w=[("V1", vb)])
                if psi_fn is not None:
                    psi_fn(h)
            ps_, pcs = sbufs[i % 3]
            qs = slice(qt * 512, (qt + 1) * 512)
            bias = psi_fn is not None and kb >= 4
            mm(ps_[:], K[0:krows, kv, kb * 128:(kb + 1) * 128], Q[0:krows, h, qs], True, not bias,
               r=[("K", kv), ("Q", h)], w=[pcs])
            if bias:
                kbl = kb - 4
                st_ = (8 * qt - 2 * kbl + 14) * 64
                mm(ps_[:], identb[:], PSI[:, 0, st_:st_ + 512], False, True, r=[("PSI", 0), "identb"], w=[pcs])

        emit_S(0)
        emit_S(1)
        for i, (h, qt, kb) in enumerate(steps):
            if i + 2 < len(steps):
                emit_S(i + 2)
            sb = i % 2
            ps_, pcs = sbufs[i % 3]
            vb = h % 2
            acc, accc = accs[h % 2]
            qs = slice(qt * 512, (qt + 1) * 512)
            act(PT[:, sb, :], ps_[:], AF.Exp, r=[pcs], w=[("PT", sb)], scale=scale)
            kl = kbs(qt)
            mm(acc[:, qs], V1[:, vb, kb, :], PT[:, sb, :], kb == kl[0], kb == kl[-1], r=[("V1", vb), ("PT", sb)], w=[accc])
            if qt == 1 and kb == kl[-1]:
                P.op("dve", lambda e, acc=acc: e.reciprocal(out=recip, in_=acc[64:128, :]), r=[accc], w=["TMP1"])
                po = (h % 2) * 64
                tt("dve", OB[po:po + 64, branch, h // 2, :], acc[0:64, :], recip, ALU.mult, r=[accc, "TMP1"],
                   w=[("OB", branch, h // 2, h % 2)])

    def ctx_transposes(src_d2, ncol, dst_fn, name):
        Q, K, V, V1, PT, PSI, STG, TMP = aviews()
        for c in range((ncol + 127) // 128):
            w_ = min(128, ncol - c * 128)
            STGc = STG[:, 0, :].rearrange("p (n d) -> p n d", d=128)
            st = STGc[:, :, 0:w_]
            dma("sp", st, src_d2[:, c * 128:c * 128 + w_].rearrange("(n p) d -> p n d", p=128), w=["STG"])
            pst, pc = nextps()
            for n in range(4):
                tp(pst[0:w_, n * 128:(n + 1) * 128], STGc[:, n, 0:w_], ident[:], r=["STG", "ident"], w=[pc])
            dst_fn(c, pst, pc, w_)

    def rope_write(pst, pc, gcol, gl, dst_fn, nrm_n, TMP):
        sq = abf(A_PT, 1024)
        act(sq, pst[:], AF.Square, r=[pc], w=[("PT", 0), ("PT", 1)])
        for th in range(2):
            mm(psO[:, th * 512:(th + 1) * 512], blk64[:], sq[:, th * 512:(th + 1) * 512], True, True, r=[("PT", 0), ("PT", 1), "blk64"], w=["psO"])
        rstd_from(psO[:], 64.0, rstd[:], "psO", "rstd")
        qn = TMP[:, 0, :]
        tt("dve", qn, pst[:], rstd[:], ALU.mult, r=[pc, "rstd"], w=["TMP0"])
        ts("dve", qn, qn, gcols[:, gcol + gl:gcol + gl + 1], None, ALU.mult, r=["TMP0", "gcols"], w=["TMP0"])
        return qn

    def gqa(l):
        Q, K, V, V1, PT, PSI, STG, TMP = aviews()

        def c_pre(slot):
            attn_masks(qmf_d, 2)
            def dst(c, pst, pc, w_):
                cp("dve", K[0:64, 0, 0:512], pst[0:64, 0:512], r=[pc], w=[("K", 0)])
                cp("dve", K[0:64, 1, 0:512], pst[64:128, 0:512], r=[pc], w=[("K", 1)])
            ctx_transposes(ckg_d[l], 128, dst, "ckg")
            dma("pool", V[:, 0:4, 0:2, :], cvg_d[l].rearrange("(n p) (h d) -> p n h d", p=128, d=64), w=[("V", 0), ("V", 1)])
        job(None, c_pre)

        def rope_finish(qn, dsts, cells):
            qb = abf(A_PT, 1024)
            cp("act", qb, qn, r=["TMP0"], w=[("PT", 0), ("PT", 1)])
            for th in range(2):
                mm(psO[:, th * 512:(th + 1) * 512], pswap[:], qb[:, th * 512:(th + 1) * 512], True, True, r=[("PT", 0), ("PT", 1), "pswap"], w=["psO"])
            t2 = TMP[:, 1, :]
            tt("dve", t2, psO[:], sing[:], ALU.mult, r=["psO", "sing"], w=["TMP1"])
            tt("pool", qn, qn, cosg[:], ALU.mult, r=["TMP0", "cosg"], w=["TMP0"])
            for hh in range(2):
                tt("dve", dsts[hh], qn[hh * 64:(hh + 1) * 64, :], t2[hh * 64:(hh + 1) * 64, :], ALU.add, r=["TMP0", "TMP1"], w=[cells[hh]])

        def ld_q(slot):
            wload(slot, [(sl3(slot, 8, 512), wsrc(w_in_d[l], C_GQ, 512))])

        def c_q(slot):
            w3 = sl3(slot, 8, 512)
            for c in range(4):
                pst, pc = nextps()
                proj_fm(slot, w3, 512, c * 128, 128, pst, pc, hT, "hT", 8)
                qn = rope_write(pst, pc, 2, l, None, 64.0, TMP)
                rope_finish(qn, [Q[0:64, 2 * c, :], Q[0:64, 2 * c + 1, :]], [("Q", 2 * c), ("Q", 2 * c + 1)])
        job(ld_q, c_q)

        def ld_k(slot):
            w3 = sl3(slot, 8, 256)
            wload(slot, [(w3, wsrc(w_in_d[l], C_GK, 256))])

        def c_k(slot):
            w3 = sl3(slot, 8, 256)
            pst, pc = nextps()
            proj_fm(slot, w3, 256, 0, 128, pst, pc, hT, "hT", 8)
            qn = rope_write(pst, pc, 4, l, None, 64.0, TMP)
            rope_finish(qn, [K[0:64, 0, 512:1536], K[0:64, 1, 512:1536]], [("K", 0), ("K", 1)])
            dma("sp", gbc[:, 0:64], gqk_d[l:l + 1, :].partition_broadcast(128), w=["gbc"])
            kraw = TMP[:, 0, :]
            vst = TMP[:, 1, :]
            for tb in range(8):
                pst, pc = nextps()
                proj_tm(slot, w3, 0, 256, tb, pst, pc)
                cp("act", kraw[:, tb * 128:(tb + 1) * 128], pst[:, 0:128], r=[pc], w=["TMP0"])
                cp("dve", vst[:, tb * 128:(tb + 1) * 128], pst[:, 128:256], r=[pc], w=["TMP1"])
            sqa = STG[:, 0:2, :].rearrange("p b n -> p (b n)")
            act(sqa, kraw, AF.Square, r=["TMP0"], w=["STG", "STG1"])
            ss = smallf[:, 0:16]
            P.op("dve", lambda e: e.reduce_sum(out=ss, in_=sqa.rearrange("p (h d) -> p h d", d=64), axis=AX.X), r=["STG", "STG1"], w=["smallf"])
            rstd_from(ss, 64.0, smallf[:, 16:32], "smallf", "smallf")
            kn3 = sqa.rearrange("p (h d) -> p h d", d=64)
            tt("dve", kn3, kraw.rearrange("p (h d) -> p h d", d=64),
               smallf[:, 16:32].rearrange("p (h o) -> p h o", o=1).to_broadcast([128, 16, 64]), ALU.mult, r=["TMP0", "smallf"], w=["STG", "STG1"])
            tt("dve", kn3, kn3, gbc[:, 0:64].rearrange("p (o d) -> p o d", o=1).to_broadcast([128, 16, 64]), ALU.mult, r=["STG", "STG1", "gbc"], w=["STG", "STG1"])
            for hh_ in range(2):
                dma("sp", ogk_d[:, l, hh_ * 128:(hh_ + 1) * 128, :].rearrange("s p d -> p s d"), sqa.rearrange("p (s h d) -> p s h d", s=4, h=2)[:, :, hh_, :], r=["STG", "STG1"], group="out")
            for hh_ in range(2):
                dma("sp", ogv_d[:, l, hh_ * 128:(hh_ + 1) * 128, :].rearrange("s p d -> p s d"), vst.rearrange("p (s h d) -> p s h d", s=4, h=2)[:, :, hh_, :], r=["TMP1"], group="out")
            cp("dve", V[:, 4:12, 0:2, :], vst.rearrange("p (t h d) -> p t h d", h=2, d=64), r=["TMP1"], w=[("V", 0), ("V", 1)])
        job(ld_k, c_k)

        def c_core(slot):
            attn_core(1, lambda h: h // 4, 96, GQ_SCALE, 64)
        job(None, c_core)

    def na(l):
        Q, K, V, V1, PT, PSI, STG, TMP = aviews()

        def c_pre(slot):
            attn_masks(qmn_d, 8)

            def dst(c, pst, pc, w_):
                cp("dve", K[0:64, 2 * c, 0:512], pst[0:64, 0:512], r=[pc], w=[("K", 2 * c)])
                cp("dve", K[0:64, 2 * c + 1, 0:512], pst[64:128, 0:512], r=[pc], w=[("K", 2 * c + 1)])
            ctx_transposes(ckn_d[l], 512, dst, "ckn")
            dma("pool", V[:, 0:4, :, :], cvn_d[l].rearrange("(n p) (h d) -> p n h d", p=128, d=64), w=[("V", h) for h in range(8)])
            prow = af32(A_TMP, 256)
            memset("dve", prow[:, :], 0.0, w=["TMP0"])
            src = bass.AP(rpb_d.tensor, l * 8 * 15 * 31 + 30, [[31, 120], [-1, 31]])
            dma("sp", prow[0:120, 48:79], src, r=["TMP0"], w=["TMP0"], slow=True)
            dstb = bass.AP(bscr.tensor, l * 120 * 64 * 128, [[64 * 128, 120], [128, 64], [1, 128]])
            dma("sp", dstb, prow[0:120, :].rearrange("p (o n) -> p o n", o=1).to_broadcast([120, 64, 128]), r=["TMP0"], w=["bscr"], group="bscr")
        job(None, c_pre)

        def ld_q(slot):
            wload(slot, [(sl3(slot, 8, 512), wsrc(w_in_d[l], C_NQ, 512))])

        def c_q(slot):
            w3 = sl3(slot, 8, 512)
            for c in range(4):
                pst, pc = nextps()
                proj_fm(slot, w3, 512, c * 128, 128, pst, pc, hT, "hT", 8)
                act(Q[0:64, 2 * c, :], pst[0:64, :], AF.Copy, r=[pc], w=[("Q", 2 * c)], scale=NA_SCALE)
                ts("dve", Q[0:64, 2 * c + 1, :], pst[64:128, :], NA_SCALE, None, ALU.mult, r=[pc], w=[("Q", 2 * c + 1)])
        job(ld_q, c_q)

        def ld_k(slot):
            wload(slot, [(sl3(slot, 8, 512), wsrc(w_in_d[l], C_NK, 512))])

        def c_k(slot):
            w3 = sl3(slot, 8, 512)
            for c in range(4):
                pst, pc = nextps()
                proj_fm(slot, w3, 512, c * 128, 128, pst, pc, hT, "hT", 8)
                cp("act", K[0:64, 2 * c, 512:1536], pst[0:64, :], r=[pc], w=[("K", 2 * c)])
                cp("dve", K[0:64, 2 * c + 1, 512:1536], pst[64:128, :], r=[pc], w=[("K", 2 * c + 1)])
            for tb in range(8):
                pst, pc = nextps()
                proj_tm(slot, w3, 0, 512, tb, pst, pc)
                cp("act", STG[:, tb % 2, :], pst[:, 0:512], r=[pc], w=[["STG", "STG1"][tb % 2]])
                seg_out(onk_d, l, tb, 512, STG[:, tb % 2, :], ["STG", "STG1"][tb % 2])
        job(ld_k, c_k)

        def ld_v(slot):
            wload(slot, [(sl3(slot, 8, 512), wsrc(w_in_d[l], C_NV, 512))])

        def c_v(slot):
            w3 = sl3(slot, 8, 512)
            for tb in range(8):
                pst, pc = nextps()
                proj_tm(slot, w3, 0, 512, tb, pst, pc)
                cp("act", STG[:, 2, :], pst[:, 0:512], r=[pc], w=["STG2"])
                seg_out(onv_d, l, tb, 512, STG[:, 2, :], "STG2")
                cp("dve", V[:, 4 + tb, :, :], pst[:, 0:512].rearrange("p (h d) -> p h d", d=64), r=[pc], w=[("V", h) for h in range(8)])
        job(ld_v, c_v)

        def psi_build(h):
            sb = 0
            pf = TMP[:, 0, :]
            p3 = pf[:, 0:960].rearrange("p (r c) -> p r c", c=64)
            for par in range(2):
                base = (l * 120 + h * 15 + 14) * 64 * 128 + 63
                src = bass.AP(bscr.tensor, base, [[127, 64], [-64 * 128, 15], [1, 64]])
                dma("sp", p3[par * 64:(par + 1) * 64, :, :], src, r=["bscr"], w=["TMP0"])
            tt("dve", p3, p3, colm[:, :].rearrange("p (o c) -> p o c", o=1).to_broadcast([128, 15, 64]), ALU.add, r=["TMP0", "colm"], w=["TMP0"])
            memset("pool", PSI[:, sb, :], 0.0, w=[("PSI", sb)])
            for par in range(2):
                r0 = 7 + par
                cp("dve", PSI[par * 64:(par + 1) * 64, sb, r0 * 64:(r0 + 15) * 64], pf[par * 64:(par + 1) * 64, 0:960], r=["TMP0"], w=[("PSI", sb)])

        def c_core(slot):
            attn_core(2, lambda h: h, 96, 1.0, 64, psi_fn=psi_build)
        job(None, c_core)

    def mla(l):
        Q, K, V, V1, PT, PSI, STG, TMP = aviews()
        mqn = abf(A_PSI, 2 * T).rearrange("p (k t) -> p k t", t=T)
        ckvT = abf(A_V1, 1536)

        def c_pre(slot):
            Qm = abf(A_Q, 8 * T).rearrange("p (h t) -> p h t", t=T)
            dma("sp", Q[96:128, :, :], qmf_d[:, :].rearrange("p (o t) -> p o t", o=1).to_broadcast([32, 8, T]), w=[("Q", h) for h in range(8)])
            dma("act", K[96:128, :, :], kmask_d[:, :].rearrange("p (o t) -> p o t", o=1).to_broadcast([32, 8, 1536]), w=[("K", h) for h in range(8)])
            memset("pool", V1[:, 1, :, 64:128], 1.0, w=[("V1", 1)])

            def dst(c, pst, pc, w_):
                cp("dve", ckvT[:, 0:512], pst[:, 0:512], r=[pc], w=["ckvT"])
            ctx_transposes(cckv_d[l], 128, dst, "cckv")

            def dst2(c, pst, pc, w_):
                for h in range(8):
                    cp("dve" if h % 2 else "act", K[64:96, h, 0:512], pst[0:32, 0:512], r=[pc], w=[("K", h)])
            ctx_transposes(ckr_d[l], 32, dst2, "ckr")
        job(None, c_pre)

        def ld_a(slot):
            w3 = sl3(slot, 8, 512)
            wload(slot, [(w3[:, :, 0:416], wsrc(w_in_d[l], C_MQA, 416))])

        def c_a(slot):
            w3 = sl3(slot, 8, 512)
            qraw = TMP
            for c in range(2):
                pst, pc = nextps()
                proj_fm(slot, w3, 512, c * 128, 128, pst, pc, hT, "hT", 8)
                cp("dve", qraw[:, c, :], pst[:], r=[pc], w=["TMP%d" % c])
                act(mqn[:, c, :], pst[:], AF.Square, r=[pc], w=[("mqn", c)])
            for th in range(2):
                for c in range(2):
                    mm(psO[:, th * 512:(th + 1) * 512], onesb[:], mqn[:, c, th * 512:(th + 1) * 512], c == 0, c == 1, r=[("mqn", c), "onesb"], w=["psO"])
            rstd_from(psO[:], 256.0, rstd[:], "psO", "rstd")
            for c in range(2):
                stt("dve", mqn[:, c, :], qraw[:, c, :], gcols[:, 6 + 2 * l + c:7 + 2 * l + c], rstd[:], ALU.mult, ALU.mult,
                    r=["TMP%d" % c, "gcols", "rstd"], w=[("mqn", c)])
            pst, pc = nextps()
            proj_fm(slot, w3, 512, 256, 128, pst, pc, hT, "hT", 8)
            sq = abf(A_PT, 1024)
            act(sq, pst[:], AF.Square, r=[pc], w=[("PT", 0), ("PT", 1)])
            for th in range(2):
                mm(psO[:, th * 512:(th + 1) * 512], onesb[:], sq[:, th * 512:(th + 1) * 512], True, True, r=[("PT", 0), ("PT", 1), "onesb"], w=["psO"])
            rstd_from(psO[:], 128.0, rstd[:], "psO", "rstd")
            stt("dve", ckvT[:, 512:1536], pst[:], gcols[:, 10 + l:11 + l], rstd[:], ALU.mult, ALU.mult, r=[pc, "gcols", "rstd"], w=["ckvT"])
            pst, pc = nextps()
            proj_fm(slot, w3, 512, 320, 96, pst, pc, hT, "hT", 8)
            krb = abf(A_PT, 1024)
            cp("dve", krb[64:96, :], pst[64:96, :], r=[pc], w=[("PT", 0), ("PT", 1)])
            for th in range(2):
                mm(psO[0:32, th * 512:(th + 1) * 512], pswap[64:96, 64:96], krb[64:96, th * 512:(th + 1) * 512], True, True,
                   r=[("PT", 0), ("PT", 1), "pswap"], w=["psO"])
            t1 = TMP[64:96, 0, :]
            t2 = TMP[64:96, 1, :]
            tt("dve", t1, pst[64:96, :], csm[64:96, :], ALU.mult, r=[pc, "csm"], w=["TMP0"])
            tt("dve", t2, psO[0:32, :], csm[0:32, :], ALU.mult, r=["psO", "csm"], w=["TMP1"])
            tt("dve", t1, t1, t2, ALU.add, r=["TMP0", "TMP1"], w=["TMP0"])
            for h in range(8):
                cp("dve" if h % 2 else "act", K[64:96, h, 512:1536], t1, r=["TMP0"], w=[("K", h)])
            dma("sp", gbc[:, 0:128], mlkv_g_d[l:l + 1, :].partition_broadcast(128), w=["gbc"])
            craw = TMP[:, 0, :]
            krst = STG[:, 2, 0:256]
            for tb in range(8):
                pst, pc = nextps()
                proj_tm(slot, w3, 256, 160, tb, pst, pc)
                cp("act", craw[:, tb * 128:(tb + 1) * 128], pst[:, 0:128], r=[pc], w=["TMP0", "TMP0"])
                cp("dve", krst[:, tb * 32:(tb + 1) * 32], pst[:, 128:160], r=[pc], w=["STG2"])
            sqa = STG[:, 0:2, :].rearrange("p b n -> p (b n)")
            act(sqa, craw, AF.Square, r=["TMP0", "TMP0"], w=["STG", "STG1"])
            ss = smallf[:, 0:8]
            P.op("dve", lambda e: e.reduce_sum(out=ss, in_=sqa.rearrange("p (t d) -> p t d", d=128), axis=AX.X), r=["STG", "STG1"], w=["smallf"])
            rstd_from(ss, 128.0, smallf[:, 16:24], "smallf", "smallf")
            c3 = sqa.rearrange("p (t d) -> p t d", d=128)
            tt("dve", c3, craw.rearrange("p (t d) -> p t d", d=128),
               smallf[:, 16:24].rearrange("p (t o) -> p t o", o=1).to_broadcast([128, 8, 128]), ALU.mult, r=["TMP0", "TMP0", "smallf"], w=["STG", "STG1"])
            tt("dve", c3, c3, gbc[:, 0:128].rearrange("p (o d) -> p o d", o=1).to_broadcast([128, 8, 128]), ALU.mult, r=["STG", "STG1", "gbc"], w=["STG", "STG1"])
            for hh_ in range(2):
                dma("sp", ockv_d[:, l, hh_ * 128:(hh_ + 1) * 128, :].rearrange("s p d -> p s d"), sqa.rearrange("p (s h d) -> p s h d", s=4, h=2)[:, :, hh_, :], r=["STG", "STG1"], group="out")
            for hh_ in range(2):
                dma("sp", okr_d[:, l, hh_ * 128:(hh_ + 1) * 128, :].rearrange("s p d -> p s d"), krst.rearrange("p (s h d) -> p s h d", s=4, h=2)[:, :, hh_, :], r=["STG2"], group="out")
        job(ld_a, c_a)

        def ld_qb(slot):
            w3 = sl3(slot, 2, 768)
            wload(slot, [(w3, wsrc(wqb_d[l], 0, 768))])

        def c_qb(slot):
            w3 = sl3(slot, 2, 768)
            for h in range(8):
                pst, pc = nextps()
                proj_fm(slot, w3, 768, h * 96, 96, pst, pc, mqn, "mqn", 2)
                cp("act", Q[0:64, h, :], pst[0:64, :], r=[pc], w=[("Q", h)])
                qrb = abf(A_PT, 1024)
                cp("dve", qrb[64:96, :], pst[64:96, :], r=[pc], w=[("PT", 0), ("PT", 1)])
                for th in range(2):
                    mm(psO[0:32, th * 512:(th + 1) * 512], pswap[64:96, 64:96], qrb[64:96, th * 512:(th + 1) * 512], True, True,
                       r=[("PT", 0), ("PT", 1), "pswap"], w=["psO"])
                t1 = TMP[64:96, 0, :]
                t2 = TMP[64:96, 1, :]
                tt("dve", t1, pst[64:96, :], csm[64:96, :], ALU.mult, r=[pc, "csm"], w=["TMP0"])
                tt("dve", t2, psO[0:32, :], csm[0:32, :], ALU.mult, r=["psO", "csm"], w=["TMP1"])
                tt("dve", Q[64:96, h, :], t1, t2, ALU.add, r=["TMP0", "TMP1"], w=[("Q", h)])
        job(ld_qb, c_qb)

        def ld_kvb(slot):
            w3 = sl3(slot, 1, 1024)
            wload(slot, [(w3, wsrc(wkvb_d[l], 0, 1024))])

        def c_kvb(slot):
            w4 = wsl[slot][:, 0:1024].rearrange("p (h s d) -> p h s d", s=2, d=64)
            for h in range(8):
                for j in range(3):
                    sb = (h * 3 + j) % 2
                    ps_ = psS[sb]
                    pcs = "psS%d" % sb
                    mm(ps_[0:64, :], w4[:, h, 0, :], ckvT[:, j * 512:(j + 1) * 512], True, True, r=[("w", slot), "ckvT"], w=[pcs])
                    cp("act" if sb else "dve", K[0:64, h, j * 512:(j + 1) * 512], ps_[0:64, :], r=[pcs], w=[("K", h)])
            for kb in range(12):
                sb = kb % 2
                ps_ = psS[sb]
                pcs = "psS%d" % sb
                mm(ps_[:].rearrange("p (h d) -> p h d", d=64), ckvT[:, kb * 128:(kb + 1) * 128], w4[:, :, 1, :], True, True,
                   r=[("w", slot), "ckvT"], w=[pcs])
                cp("act" if sb else "dve", V[:, kb, :, :], ps_[:].rearrange("p (h d) -> p h d", d=64), r=[pcs], w=[("V", h) for h in range(8)])
        job(ld_kvb, c_kvb)

        def c_core(slot):
            memset("pool", V1[:, 0, :, 64:128], 1.0, w=[("V1", 0), "ckvT"])
            attn_core(3, lambda h: h, 128, ML_SCALE, 96)
        job(None, c_core)

    def merge_out(l):
        sg = abf(0, 4 * T).rearrange("p (n t) -> p n t", t=T)
        macc = af32(4 * T, 8 * T).rearrange("p (c t) -> p c t", t=T)
        mtmp = af32(12 * T, 4 * T).rearrange("p (b t) -> p b t", t=T)
        merged = abf(16 * T, 8 * T).rearrange("p (c t) -> p c t", t=T)
        for dcg in range(2):
            for n in range(4):
                def ld_g(slot, dcg=dcg, n=n):
                    wload(slot, [(sl3(slot, 8, 512), wsrc(w_in_d[l], C_GATE + n * 1024 + dcg * 512, 512))])

                def c_g(slot, dcg=dcg, n=n):
                    w3 = sl3(slot, 8, 512)
                    for m in range(4):
                        pst, pc = nextps()
                        proj_fm(slot, w3, 512, m * 128, 128, pst, pc, hT, "hT", 8)
                        act(sg[:, m, :], pst[:], AF.Sigmoid, r=[pc], w=[("sg", m)])
                job(ld_g, c_g)

                def ld_b(slot, dcg=dcg, n=n):
                    w3 = sl3(slot, 4, 512)
                    wload(slot, [(w3, wbr_d[l, n][:, dcg * 512:(dcg + 1) * 512].rearrange("(k p) c -> p k c", p=128))])

                def c_b(slot, dcg=dcg, n=n):
                    w3 = sl3(slot, 4, 512)
                    for m in range(4):
                        pst, pc = nextps()
                        for th in range(2):
                            for k in range(4):
                                mm(pst[:, th * 512:(th + 1) * 512], w3[:, k, m * 128:(m + 1) * 128], OB[:, n, k, th * 512:(th + 1) * 512],
                                   k == 0, k == 3, r=[("w", slot)] + obcells(n, k), w=[pc])
                        if n == 0:
                            tt("dve", macc[:, m, :], pst[:], sg[:, m, :], ALU.mult, r=[pc, ("sg", m)], w=[("macc", m)])
                        else:
                            tt("dve", mtmp[:, m % 2, :], pst[:], sg[:, m, :], ALU.mult, r=[pc, ("sg", m)], w=[("mtmp", m % 2)])
                            if n < 3:
                                tt("pool", macc[:, m, :], macc[:, m, :], mtmp[:, m % 2, :], ALU.add, r=[("macc", m), ("mtmp", m % 2)], w=[("macc", m)])
                            else:
                                tt("pool", merged[:, dcg * 4 + m, :], macc[:, m, :], mtmp[:, m % 2, :], ALU.add,
                                   r=[("macc", m), ("mtmp", m % 2)], w=[("merged", dcg * 4 + m)])
                job(ld_b, c_b)
        for g in range(2):
            def ld_o(slot, g=g):
                wload(slot, [(sl3(slot, 8, 512), wsrc(wout_d[l], g * 512, 512))])

            def c_o(slot, g=g):
                w3 = sl3(slot, 8, 512)
                for m in range(4):
                    mc = g * 4 + m
                    pst, pc = nextps()
                    proj_fm(slot, w3, 512, m * 128, 128, pst, pc, merged, "merged", 8)
                    stt("dve", xT[:, mc, :], pst[:], modT[:, l, 16 + mc:17 + mc], xT[:, mc, :], ALU.mult, ALU.add,
                        r=[pc, "modT", ("xT", mc)], w=[("xT", mc)])
            job(ld_o, c_o)

    def obcells(n, k):
        if n == 0:
            return [("OB", 0, k)]
        return [("OB", n, k, 0), ("OB", n, k, 1)]

    def ffn(l):
        uT = abf(0, 22 * T).rearrange("p (j t) -> p j t", t=T)
        silb = abf(22 * T, 2 * T).rearrange("p (b t) -> p b t", t=T)
        for jp in range(11):
            def ld(slot, jp=jp):
                w3 = sl3(slot, 8, 512)
                wload(slot, [(w3[:, :, 0:256], wsrc(wfi_d[l], jp * 256, 256)), (w3[:, :, 256:512], wsrc(wfi_d[l], FFN + jp * 256, 256))])

            def comp(slot, jp=jp):
                w3 = sl3(slot, 8, 512)
                for jj in range(2):
                    j = jp * 2 + jj
                    (pa, pac), (pg, pgc) = [((ps0, "ps0"), (ps1, "ps1")), ((psSf, "psSf"), (psO, "psO"))][jj]
                    silj = silb[:, jj, :]
                    proj_fm(slot, w3, 512, 256 + jj * 128, 128, pg, pgc, hT, "hT", 8)
                    proj_fm(slot, w3, 512, jj * 128, 128, pa, pac, hT, "hT", 8)
                    act(silj, pg[:], AF.Silu, r=[pgc], w=[("sil", jj)])
                    tt("dve", uT[:, j, :], pa[:], silj, ALU.mult, r=[pac, ("sil", jj)], w=[("uT", j)])
            job(ld, comp)
        tiles = [(ps0, "ps0"), (ps1, "ps1"), (psSf, "psSf"), (psO, "psO")]
        for mp in range(4):
            for kh in range(2):
                def ld(slot, mp=mp, kh=kh):
                    w3 = sl3(slot, 11, 256)
                    wload(slot, [(w3, wfo_d[l][kh * 1408:(kh + 1) * 1408, mp * 256:(mp + 1) * 256].rearrange("(k p) c -> p k c", p=128))])

                def comp(slot, mp=mp, kh=kh):
                    w3 = sl3(slot, 11, 256)
                    for mm_ in range(2):
                        m = mp * 2 + mm_
                        pst, pc = tiles[(mp % 2) * 2 + mm_]
                        for th in range(2):
                            for k in range(11):
                                kk = kh * 11 + k
                                mm(pst[:, th * 512:(th + 1) * 512], w3[:, k, mm_ * 128:(mm_ + 1) * 128], uT[:, kk, th * 512:(th + 1) * 512],
                                   kk == 0, kk == 21, r=[("w", slot), ("uT", kk)], w=[pc])
                        if kh == 1:
                            stt("dve", xT[:, m, :], pst[:], modT[:, l, 40 + m:41 + m], xT[:, m, :], ALU.mult, ALU.add,
                                r=[pc, "modT", ("xT", m)], w=[("xT", m)])
                job(ld, comp)

    def fence_job():
        job(None, lambda slot: P.fence())

    for l in range(L):
        if l == 1:
            drain(100)
        norm_mod(l, 0)
        auto_defer[0] = True
        hgrn(l)
        fence_job()
        gqa(l)
        fence_job()
        na(l)
        fence_job()
        mla(l)
        auto_defer[0] = False
        drain(100 if l == 0 and len(deferred) > 12 else 0)
        fence_job()
        set_wide(True)
        merge_out(l)
        set_wide(False)
        norm_mod(l, 1)
        ffn(l)
    fence_job()

    def final(slot):
        for c in range(8):
            act(hT[:, c, :], xT[:, c, :], AF.Square, r=[("xT", c)], w=[("hT", c)])
        for th in range(2):
            for c in range(8):
                mm(ps0[:, th * 512:(th + 1) * 512], onesb[:], hT[:, c, th * 512:(th + 1) * 512], c == 0, c == 7, r=[("hT", c), "onesb"], w=["ps0"])
        rstd_from(ps0[:], float(D), rstd[:], "ps0", "rstd")
        ytok = af32(0, 2 * 8 * T).rearrange("p (n d) -> p n d", d=D)
        yn = af32(2 * 8 * T, 2 * 2 * T).rearrange("p (b t) -> p b t", t=T)
        for c in range(8):
            stt("dve", yn[:, c % 2, :], xT[:, c, :], fgc[:, c:c + 1], rstd[:], ALU.mult, ALU.mult, r=[("xT", c), "fgc", "rstd"], w=[("yn", c % 2)])
            pst, pc = (ps1, "ps1") if c % 2 == 0 else (psO, "psO")
            for tb in range(8):
                tp(pst[:, tb * 128:(tb + 1) * 128], yn[:, c % 2, tb * 128:(tb + 1) * 128], ident[:], r=[("yn", c % 2), "ident"], w=[pc])
            for tb in range(8):
                cp("act" if tb % 2 else "dve", ytok[:, tb, c * 128:(c + 1) * 128], pst[:, tb * 128:(tb + 1) * 128], r=[pc], w=[("ytok", tb)])
        for tb in range(8):
            dma("sp", y_d[tb * 128:(tb + 1) * 128, :], ytok[:, tb, :], r=[("ytok", tb)], group="out")
    job(None, final)

    wj = [i for i, (ld, _) in enumerate(jobs) if ld is not None]
    widx_of = {ji: k for k, ji in enumerate(wj)}
    lp = 0
    done_w = 0
    import os
    kstop = int(os.environ.get("KSTOP", "100000"))
    if kstop < len(jobs):
        print("JOBS:", [(i, n) for i, n in enumerate(jobnames)][:kstop + 2])
    for ji, (ld, comp) in enumerate(jobs):
        if ji >= kstop:
            break
        while lp < len(wj) and lp < done_w + NSLOT and wj[lp] < kstop:
            jobs[wj[lp]][0](lp % NSLOT)
            lp += 1
        if ld is not None:
            comp(widx_of[ji] % NSLOT)
            done_w += 1
        else:
            comp(None)

    if os.environ.get("KDUMP"):
        dbg = nc.dram_tensor("dbg", [128, 16, T], F32, kind="ExternalOutput").ap()
        P.fence()
        dma("pool", dbg[:, :, :], OB[:, :, :, :].rearrange("p n c t -> p (n c) t"), r=[], group="out")
    print("SBUF bytes remaining:", nc.sbuf_bytes_remaining)
    P.emit(nc, stack)
    stack.close()
    return nc


def _consts(is_sample):
    c = {}
    c["ident"] = np.eye(128, dtype=np.float32)
    b = np.zeros((128, 128), np.float32)
    b[0:64, 0:64] = 1
    b[64:128, 64:128] = 1
    c["blk64"] = b
    p = np.zeros((128, 128), np.float32)
    for i in range(64):
        p[2 * i + 1, 2 * i] = -1.0
        p[2 * i, 2 * i + 1] = 1.0
    c["pswap"] = p
    s = np.arange(128)
    same = (s[:, None] // 64) == (s[None, :] // 64)
    trif = (same & (s[:, None] <= s[None, :])).astype(np.float32)
    trib = (same & (s[:, None] >= s[None, :])).astype(np.float32)
    c["trif"] = np.concatenate([trif, np.eye(128, dtype=np.float32)], axis=1)
    c["trib"] = np.concatenate([trib, np.eye(128, dtype=np.float32)], axis=1)
    sameh = (s[:, None] // 32) == (s[None, :] // 32)
    early = (s % 64) < 32
    mh_f = (sameh & (s[:, None] <= s[None, :])).astype(np.float32)
    mh_b = (sameh & (s[:, None] >= s[None, :])).astype(np.float32)
    mx_f = (same & early[:, None] & (~early)[None, :]).astype(np.float32)
    mx_b = (same & (~early)[:, None] & early[None, :]).astype(np.float32)
    c["hmask"] = np.concatenate([mh_f, mh_b, mx_f, mx_b], axis=1)
    km = np.zeros((32, 1536), np.float32)
    km[16, 0:512] = 1
    t = np.arange(1024)
    km[t // 64, 512 + t] = 1
    c["kmask"] = km.astype(ml_dtypes.bfloat16)
    qf = np.zeros((32, 1024), np.float32)
    qn = np.zeros((32, 1024), np.float32)
    rq = t // 64
    if is_sample:
        start = np.clip(rq - 4, 0, 8)
        for j in range(16):
            ok = (j >= start) & (j < start + 8)
            qn[j, ~ok] = -BIG
    else:
        for j in range(16):
            ok = (rq // 4) == (j // 4)
            qf[j, ~ok] = -BIG
        qf[16, :] = -BIG
        qn[:] = qf
    c["qmask_full"] = qf.astype(ml_dtypes.bfloat16)
    c["qmask_na"] = qn.astype(ml_dtypes.bfloat16)
    cm = np.zeros((128, 64), np.float32)
    if is_sample:
        col = np.arange(64)
        cs = np.clip(col - 8, 0, 48)
        ok = (col[None, :] >= cs[:, None]) & (col[None, :] < cs[:, None] + 16)
        m = np.where(ok.T, 0.0, -BIG).astype(np.float32)
        cm[0:64] = m
        cm[64:128] = m
    c["colmask"] = cm

    def rope(rot_dim):
        nf = rot_dim // 4
        inv = 10000.0 ** (-np.arange(nf, dtype=np.float32) / nf)
        row = (t // 64).astype(np.float32)[:, None]
        colv = (t % 64).astype(np.float32)[:, None]
        ang = np.concatenate([row * inv, colv * inv], axis=-1)
        return np.cos(ang).astype(np.float32), np.sin(ang).astype(np.float32)

    cg = np.ones((128, 1024), np.float32)
    sg = np.zeros((128, 1024), np.float32)
    csm = np.zeros((128, 1024), np.float32)
    csm[64:96] = 1.0
    if is_sample:
        co, si = rope(64)
        cg[:] = np.tile(np.repeat(co.T, 2, axis=0), (2, 1))
        sg[:] = np.tile(np.repeat(si.T, 2, axis=0), (2, 1))
        co, si = rope(32)
        csm[64:96] = np.repeat(co.T, 2, axis=0)
        csm[0:32] = np.repeat(si.T, 2, axis=0)
    c["cos_g"] = cg
    c["sin_g"] = sg
    c["cossin_m"] = csm
    c["keep"] = np.full((128, 1), 1.0 if is_sample else 0.0, np.float32)
    return c


_NC_CACHE = {}


def kernel(**inp):
    f = lambda a: np.ascontiguousarray(np.asarray(a, dtype=np.float32))
    inp = {k: f(v) for k, v in inp.items()}
    if "nc" not in _NC_CACHE:
        _NC_CACHE["nc"] = build()
    nc = _NC_CACHE["nc"]
    shared = {k: inp[k] for k in ["w_ada", "b_ada", "w_in", "hg_lb_logits", "hg_gain", "gq_q_gain", "gq_k_gain",
                                  "ml_q_a_gain", "ml_kv_a_gain", "ml_w_q_b", "ml_w_kv_b", "w_branch", "w_out",
                                  "w_ffn_in", "w_ffn_out"]}
    shared["final_gain"] = inp["final_gain"].reshape(1, D)
    cs, cp_ = _consts(True), _consts(False)
    in_maps = []
    for core in range(8):
        m = dict(shared)
        if core < 4:
            b = core
            m.update(cs)
            m["x"] = inp["x_sample"][b]
            m["cond"] = inp["c"][b:b + 1]
            m["na_rpb"] = inp["na_rpb"]
            m["state0"] = inp["state_hgrn"][b]
            m["ck_gqa"] = inp["cache_gqa_k"][b].reshape(L, 512, 128)
            m["cv_gqa"] = inp["cache_gqa_v"][b].reshape(L, 512, 128)
            m["ck_na"] = inp["cache_na_k"][b].reshape(L, 512, 512)
            m["cv_na"] = inp["cache_na_v"][b].reshape(L, 512, 512)
            m["c_ckv"] = inp["cache_mla_ckv"][b]
            m["c_kr"] = inp["cache_mla_krope"][b]
        else:
            j = core - 4
            m.update(cp_)
            m["x"] = inp["x_prompt"][4 * j:4 * j + 4].reshape(T, D)
            m["cond"] = inp["c_ctx"].reshape(1, D)
            m["na_rpb"] = np.zeros((L, 8, 15, 31), np.float32)
            m["state0"] = np.zeros((L, 2, 4, 128, 128), np.float32)
            m["ck_gqa"] = np.zeros((L, 512, 128), np.float32)
            m["cv_gqa"] = np.zeros((L, 512, 128), np.float32)
            m["ck_na"] = np.zeros((L, 512, 512), np.float32)
            m["cv_na"] = np.zeros((L, 512, 512), np.float32)
            m["c_ckv"] = np.zeros((L, 512, 128), np.float32)
            m["c_kr"] = np.zeros((L, 512, 32), np.float32)
        in_maps.append({k: np.ascontiguousarray(v) for k, v in m.items()})
    import os
    kc = os.environ.get("KCORES")
    if kc:
        ids = [int(v) for v in kc.split(",")]
        res = run_bass_kernel_spmd(nc, [in_maps[i] for i in ids], core_ids=list(range(len(ids))), trace=bool(os.environ.get("KTRACE")))
        print("EXEC_NS", res.exec_time_ns)
        return {i: r for i, r in zip(ids, res.results)}
    res = run_bass_kernel_spmd(nc, in_maps, core_ids=list(range(8)))
    R = res.results
    y_sample = np.stack([R[b]["y"] for b in range(4)], 0)
    y_prompt = np.concatenate([R[4 + j]["y"].reshape(4, 256, D) for j in range(4)], 0)
    cat = lambda name, shp: np.concatenate([R[4 + j][name] for j in range(4)], 0).reshape(shp)
    return (y_prompt.astype(np.float32), y_sample.astype(np.float32),
            cat("o_state", (16, L, 2, 4, 128, 128)),
            cat("o_gqk", (16, L, 256, 2, 64)), cat("o_gqv", (16, L, 256, 2, 64)),
            cat("o_nak", (16, L, 256, 8, 64)), cat("o_nav", (16, L, 256, 8, 64)),
            cat("o_ckv", (16, L, 256, 128)), cat("o_kr", (16, L, 256, 32)))
```

```python
import contextlib
import numpy as np
import ml_dtypes
import concourse.bass as bass
import concourse.mybir as mybir
from concourse.bass_utils import run_bass_kernel_spmd

F32 = mybir.dt.float32
BF16 = mybir.dt.bfloat16
AF = mybir.ActivationFunctionType
ALU = mybir.AluOpType
AX = mybir.AxisListType

D = 1024
T = 1024
L = 2
NTB = 8
EPS = 1e-6
IN_W = 9376
FFN = 2816
BIG = 30000.0
C_HQ, C_HFF, C_HFB, C_HI, C_HG = 0, 512, 1024, 1536, 2048
C_GQ, C_GK, C_GV = 2560, 3072, 3200
C_NQ, C_NK, C_NV = 3328, 3840, 4352
C_MQA, C_MKVA, C_MKR, C_GATE = 4864, 5120, 5248, 5280
GQ_SCALE = 64 ** -0.5
NA_SCALE = 64 ** -0.5
ML_SCALE = 96 ** -0.5
SERIAL = False
NSLOT = 3
SLOTW = 4096
ARENA = 39936


PSUMC = {"ps0", "ps1", "psS0", "psS1", "psO", "psSf"}


class Prog:
    def __init__(self, serial):
        self.ops = []
        self.lastw = {}
        self.readers = {}
        self.gcount = {}
        self.serial = serial
        self.cap = None

    def capture(self, f):
        self.cap = []
        f()
        c, self.cap = self.cap, None
        return c

    def replay(self, calls):
        for c in calls:
            self.op(*c)

    def replay_interleaved(self, a, b):
        na, nb = len(a), len(b)
        j = 0
        for i, c in enumerate(a):
            self.op(*c)
            while j < nb and (j + 1) * na <= (i + 1) * nb:
                self.op(*b[j])
                j += 1
        while j < nb:
            self.op(*b[j])
            j += 1

    def op(self, eng, fn, r=(), w=(), group=None):
        if self.cap is not None:
            self.cap.append((eng, fn, list(r), list(w), group))
            return -1
        i = len(self.ops)
        deps = set()
        gdeps = {}

        def add(j):
            o = self.ops[j]
            if o["group"] is not None:
                gdeps[o["group"]] = max(gdeps.get(o["group"], 0), self.gcount[o["group"]])
            else:
                deps.add(j)

        r = [x for c in r for x in ([("w", c[1], i) for i in range(4)] if (isinstance(c, tuple) and len(c) == 2 and c[0] == "w") else [c])]
        if group is not None and self.gcount.get(group, 0) > 0 and not self.serial:
            gdeps[group] = self.gcount[group]
        psc = [c for c in r if c in PSUMC]
        r = [c for c in r if c not in PSUMC] + ["phase"]
        w = list(w) + psc
        if self.serial:
            if i > 0:
                add(i - 1)
        else:
            for c in r:
                j = self.lastw.get(c)
                if j is not None:
                    add(j)
            for c in w:
                j = self.lastw.get(c)
                if j is not None:
                    add(j)
                for j in self.readers.get(c, ()):
                    add(j)
        for c in r:
            self.readers.setdefault(c, []).append(i)
        for c in w:
            self.lastw[c] = i
            self.readers[c] = []
        if group is not None:
            self.gcount[group] = self.gcount.get(group, 0) + 1
        self.ops.append(dict(eng=eng, fn=fn, deps=deps, gdeps=gdeps, group=group, signal=False, val=0))
        return i

    def fence(self):
        self.op("sp", lambda e: e.nop(), r=(), w=["phase"])

    def check(self):
        ops = self.ops
        engs = ["pe", "act", "dve", "pool", "sp"]
        q = {e: [i for i, o in enumerate(ops) if o["eng"] == e] for e in engs}
        ptr = {e: 0 for e in engs}
        done = [False] * len(ops)
        gdone = {}
        prog = True
        while prog:
            prog = False
            for e in engs:
                while ptr[e] < len(q[e]):
                    i = q[e][ptr[e]]
                    o = ops[i]
                    if all(done[j] for j in o["deps"]) and all(gdone.get(g, 0) >= c for g, c in o["gdeps"].items()):
                        done[i] = True
                        if o["group"] is not None:
                            gdone[o["group"]] = gdone.get(o["group"], 0) + 1
                        ptr[e] += 1
                        prog = True
                    else:
                        break
        stuck = {e: q[e][ptr[e]] for e in engs if ptr[e] < len(q[e])}
        for e, i in stuck.items():
            o = ops[i]
            print("STUCK", e, i, [j for j in o["deps"] if not done[j]], {g: (c, gdone.get(g, 0)) for g, c in o["gdeps"].items() if gdone.get(g, 0) < c})
        return not stuck

    def emit(self, nc, stack):
        assert self.check(), "deadlock in schedule"
        ops = self.ops
        for o in ops:
            for j in o["deps"]:
                if not (ops[j]["eng"] == "pe" and o["eng"] == "pe"):
                    ops[j]["signal"] = True
        engs = ["pe", "act", "dve", "pool", "sp"]
        cnt = {e: 0 for e in engs}
        gidx = {}
        for o in ops:
            if o["group"] is not None:
                gidx[o["group"]] = gidx.get(o["group"], 0) + 1
                o["val"] = 16 * gidx[o["group"]]
            elif o["signal"]:
                cnt[o["eng"]] += 1
                o["val"] = cnt[o["eng"]]
        esem = {e: stack.enter_context(nc.semaphore("es_" + e)) for e in engs}
        gsem = {g: stack.enter_context(nc.semaphore("gs_" + str(g))) for g in self.gcount}
        block = stack.enter_context(nc.Block())
        total_g = {g: 16 * n for g, n in self.gcount.items()}

        def run(engname, e):
            waited = {}
            for o in ops:
                if o["eng"] != engname:
                    continue
                need = {}
                for j in o["deps"]:
                    oj = ops[j]
                    if oj["eng"] == "pe" and engname == "pe":
                        continue
                    k = ("e", oj["eng"])
                    need[k] = max(need.get(k, 0), oj["val"])
                for g, c in o["gdeps"].items():
                    k = ("g", g)
                    need[k] = max(need.get(k, 0), 16 * c)
                for k, v in need.items():
                    if waited.get(k, 0) >= v:
                        continue
                    waited[k] = v
                    sem = esem[k[1]] if k[0] == "e" else gsem[k[1]]
                    e.wait_ge(sem, v)
                ins = o["fn"](e)
                if o["group"] is not None:
                    ins.then_inc(gsem[o["group"]], 16)
                elif o["signal"]:
                    ins.then_inc(esem[engname], 1)
            if engname == "sp":
                for g, v in total_g.items():
                    e.wait_ge(gsem[g], v)
                for en in engs:
                    if en != "sp" and cnt[en] > 0:
                        e.wait_ge(esem[en], cnt[en])

        @block.tensor
        def _(e):
            run("pe", e)

        @block.scalar
        def _(e):
            run("act", e)

        @block.vector
        def _(e):
            run("dve", e)

        @block.gpsimd
        def _(e):
            run("pool", e)

        @block.sync
        def _(e):
            run("sp", e)


def build(debug=False):
    nc = bass.Bass("TRN2", target_bir_lowering=False)
    P = Prog(SERIAL)
    stack = contextlib.ExitStack()
    din = {}
    dout = {}

    def DI(name, shape):
        din[name] = nc.dram_tensor(name, list(shape), F32, kind="ExternalInput").ap()
        return din[name]

    def DO(name, shape):
        dout[name] = nc.dram_tensor(name, list(shape), F32, kind="ExternalOutput").ap()
        return dout[name]

    x_d = DI("x", [T, D])
    cond_d = DI("cond", [1, D])
    w_ada_d = DI("w_ada", [L, D, 6 * D])
    b_ada_d = DI("b_ada", [L, 6 * D])
    w_in_d = DI("w_in", [L, D, IN_W])
    lbl_d = DI("hg_lb_logits", [L, 2, 512])
    hg_gain_d = DI("hg_gain", [L, 128])
    gqq_d = DI("gq_q_gain", [L, 64])
    gqk_d = DI("gq_k_gain", [L, 64])
    rpb_d = DI("na_rpb", [L, 8, 15, 31])
    mlq_g_d = DI("ml_q_a_gain", [L, 256])
    mlkv_g_d = DI("ml_kv_a_gain", [L, 128])
    wqb_d = DI("ml_w_q_b", [L, 256, 768])
    wkvb_d = DI("ml_w_kv_b", [L, 128, 1024])
    wbr_d = DI("w_branch", [L, 4, 512, D])
    wout_d = DI("w_out", [L, D, D])
    wfi_d = DI("w_ffn_in", [L, D, 2 * FFN])
    wfo_d = DI("w_ffn_out", [L, FFN, D])
    fg_d = DI("final_gain", [1, D])
    st0_d = DI("state0", [L, 2, 4, 128, 128])
    ckg_d = DI("ck_gqa", [L, 512, 128])
    cvg_d = DI("cv_gqa", [L, 512, 128])
    ckn_d = DI("ck_na", [L, 512, 512])
    cvn_d = DI("cv_na", [L, 512, 512])
    cckv_d = DI("c_ckv", [L, 512, 128])
    ckr_d = DI("c_kr", [L, 512, 32])
    ident_d = DI("ident", [128, 128])
    blk64_d = DI("blk64", [128, 128])
    pswap_d = DI("pswap", [128, 128])
    trif_d = DI("trif", [128, 256])
    trib_d = DI("trib", [128, 256])
    hmask_d = DI("hmask", [128, 512])
    def DIB(name, shape):
        din[name] = nc.dram_tensor(name, list(shape), BF16, kind="ExternalInput").ap()
        return din[name]

    kmask_d = DIB("kmask", [32, 1536])
    qmf_d = DIB("qmask_full", [32, T])
    qmn_d = DIB("qmask_na", [32, T])
    colm_d = DI("colmask", [128, 64])
    cosg_d = DI("cos_g", [128, T])
    sing_d = DI("sin_g", [128, T])
    csm_d = DI("cossin_m", [128, T])
    keep_d = DI("keep", [128, 1])

    y_d = DO("y", [T, D])
    ost_d = DO("o_state", [4, L, 2, 4, 128, 128])
    ogk_d = DO("o_gqk", [4, L, 256, 128])
    ogv_d = DO("o_gqv", [4, L, 256, 128])
    onk_d = DO("o_nak", [4, L, 256, 512])
    onv_d = DO("o_nav", [4, L, 256, 512])
    ockv_d = DO("o_ckv", [4, L, 256, 128])
    okr_d = DO("o_kr", [4, L, 256, 32])
    bscr = nc.dram_tensor("bscr", [L * 120, 64 * 128], F32, kind="Internal").ap()

    def SB(name, shape, dt):
        return stack.enter_context(nc.sbuf_tensor("sb_" + name, list(shape), dt))

    def PS(name, shape):
        return stack.enter_context(nc.psum_tensor("pt_" + name, list(shape), F32))

    xT = SB("xT", [128, 8, T], F32)
    hT = SB("hT", [128, 8, T], BF16)
    wsl = [SB("wsl%d" % i, [128, SLOTW], BF16) for i in range(NSLOT)]
    OB = SB("OB", [128, 4, 4, T], BF16)
    arena = SB("arena", [128, ARENA], BF16)
    ident = SB("identf", [128, 128], F32)
    identb = SB("identb", [128, 128], BF16)
    onesb = SB("onesb", [128, 128], BF16)
    blk64 = SB("blk64b", [128, 128], BF16)
    pswap = SB("pswapb", [128, 128], BF16)
    trif = SB("triff", [128, 256], F32)
    trib = SB("tribf", [128, 256], F32)
    hmask = SB("hmask", [128, 512], F32)
    cosg = SB("cosg", [128, T], BF16)
    sing = SB("sing", [128, T], BF16)
    csm = SB("csm", [128, T], BF16)
    colm = SB("colm", [128, 64], F32)
    keep = SB("keep", [128, 1], F32)
    modT = SB("modT", [128, L, 48], F32)
    opsc = SB("opsc", [128, L, 2, 8], F32)
    badaT = SB("badaT", [128, L, 48], F32)
    condT = SB("condT", [128, 8], F32)
    condb = SB("condb", [128, 8], BF16)
    gcols = SB("gcols", [128, 16], F32)
    lbt = SB("lbt", [128, 2, 512], F32)

    rstd = SB("rstd", [128, T], F32)
    gbc = SB("gbc", [128, 512], F32)
    smallf = SB("smallf", [128, 256], F32)

    ps0 = PS("ps0", [128, 1024])
    ps1 = PS("ps1", [128, 1024])
    psSf = PS("psS", [128, 1024])
    psS = [psSf[:, 0:512], psSf[:, 512:1024]]
    psO = PS("psO", [128, 1024])

    def af32(off, n):
        return arena[:, off:off + n].bitcast(F32)

    def abf(off, n):
        return arena[:, off:off + n]

    cnt_q = [0]

    def dma(eng, out, in_, r=(), w=(), group=None, slow=False):
        if group is None:
            cnt_q[0] += 1
            group = "d%s%d" % (eng, cnt_q[0] % 8)
        elif group == "out":
            cnt_q[0] += 1
            group = "out_%s%d" % (eng, cnt_q[0] % 8)
        kw = dict(allow_slow_non_contiguous=True) if slow else {}
        return P.op(eng, lambda e: e.dma_start(out=out, in_=in_, **kw), r=r, w=w, group=group)

    def act(out, in_, func, r=(), w=(), bias=None, scale=None):
        kw = {}
        if bias is not None:
            kw["bias"] = bias
        if scale is not None:
            kw["scale"] = scale
        return P.op("act", lambda e: e.activation(out=out, in_=in_, func=func, **kw), r=r, w=w)

    def tt(eng, out, in0, in1, op, r=(), w=()):
        return P.op(eng, lambda e: e.tensor_tensor(out=out, in0=in0, in1=in1, op=op), r=r, w=w)

    def ts(eng, out, in0, s1, s2, op0, op1=None, r=(), w=()):
        if op1 is None:
            return P.op(eng, lambda e: e.tensor_scalar(out=out, in0=in0, scalar1=s1, scalar2=None, op0=op0), r=r, w=w)
        return P.op(eng, lambda e: e.tensor_scalar(out=out, in0=in0, scalar1=s1, scalar2=s2, op0=op0, op1=op1), r=r, w=w)

    def stt(eng, out, in0, scalar, in1, op0, op1, r=(), w=()):
        eng = "dve"
        return P.op(eng, lambda e: e.scalar_tensor_tensor(out=out, in0=in0, scalar=scalar, in1=in1, op0=op0, op1=op1), r=r, w=w)

    def cp(eng, out, in_, r=(), w=()):
        if eng == "act":
            return P.op(eng, lambda e: e.activation(out=out, in_=in_, func=AF.Copy), r=r, w=w)
        return P.op(eng, lambda e: e.tensor_copy(out=out, in_=in_), r=r, w=w)

    def mm(out, lhsT, rhs, start, stop, r=(), w=()):
        return P.op("pe", lambda e: e.matmul(out, lhsT, rhs, start=start, stop=stop, skip_group_check=True), r=r, w=w)

    def tp(out, in_, idt, r=(), w=()):
        return P.op("pe", lambda e: e.transpose(out, in_, idt), r=r, w=w)

    def memset(eng, ap, v, w=()):
        return P.op(eng, lambda e: e.memset(ap, v), w=w)

    def rstd_from(ps_ap, n, out_ap, pc, oc):
        act(out_ap, ps_ap, AF.Ln, r=[pc], w=[oc], bias=epsc[:, 0:1], scale=1.0 / n)
        act(out_ap, out_ap, AF.Exp, r=[oc], w=[oc], scale=-0.5)

    epsc = SB("epsc", [128, 1], F32)
    memset("dve", epsc[:], EPS, w=["epsc"])
    onec = SB("onec", [128, 1], F32)
    memset("dve", onec[:], 1.0, w=["onec"])
    memset("dve", onesb[:], 1.0, w=["onesb"])

    def load_const(dst, src, name, cast_to=None):
        if cast_to is None:
            dma("sp", dst[:], src, w=[name])
        else:
            dma("pool", cast_to[:], src, w=[name])

    load_const(ident, ident_d[:, :], "ident")
    load_const(None, ident_d[:, :], "identb", cast_to=identb)
    load_const(None, blk64_d[:, :], "blk64", cast_to=blk64)
    load_const(None, pswap_d[:, :], "pswap", cast_to=pswap)
    load_const(trif, trif_d[:, :], "trif")
    load_const(trib, trib_d[:, :], "trib")
    load_const(hmask, hmask_d[:, :], "hmask")
    load_const(None, cosg_d[:, :], "cosg", cast_to=cosg)
    load_const(None, sing_d[:, :], "sing", cast_to=sing)
    load_const(None, csm_d[:, :], "csm", cast_to=csm)
    load_const(colm, colm_d[:, :], "colm")
    load_const(keep, keep_d[:, :], "keep")
    dma("sp", condT[:], cond_d[0, :].rearrange("(c p) -> p c", p=128), w=["condT"], slow=True)
    for l in range(L):
        dma("sp", badaT[:, l, :], b_ada_d[l, :].rearrange("(c p) -> p c", p=128), w=["badaT"], slow=True)
    fgc = SB("fgc", [128, 8], F32)
    dma("sp", fgc[:], fg_d[0, :].rearrange("(c p) -> p c", p=128), w=["fgc"], slow=True)
    for l in range(L):
        dma("sp", gcols[:, l:l + 1], hg_gain_d[l, :].rearrange("(p o) -> p o", o=1), w=["gcols"], slow=True)
        for hh in range(2):
            dma("sp", gcols[hh * 64:(hh + 1) * 64, 2 + l:3 + l], gqq_d[l, :].rearrange("(p o) -> p o", o=1), w=["gcols"], slow=True)
            dma("sp", gcols[hh * 64:(hh + 1) * 64, 4 + l:5 + l], gqk_d[l, :].rearrange("(p o) -> p o", o=1), w=["gcols"], slow=True)
        dma("sp", gcols[:, 6 + 2 * l:8 + 2 * l], mlq_g_d[l, :].rearrange("(c p) -> p c", p=128), w=["gcols"], slow=True)
        dma("sp", gcols[:, 10 + l:11 + l], mlkv_g_d[l, :].rearrange("(p o) -> p o", o=1), w=["gcols"], slow=True)
    for dr in range(2):
        dma("sp", lbt[:, dr, :], lbl_d[1, dr:dr + 1, :].partition_broadcast(128), w=["lbt"])
        dma("sp", rstd[:, dr * 512:(dr + 1) * 512], lbl_d[0, dr:dr + 1, :].partition_broadcast(128), w=["rstd"])
    tt("dve", lbt[:, :, :], lbt[:, :, :], rstd[:, :].rearrange("p (d n) -> p d n", n=512), ALU.subtract, r=["lbt", "rstd"], w=["lbt"])
    act(lbt[:, :, :], lbt[:, :, :], AF.Sigmoid, r=["lbt"], w=["lbt"])

    jobs = []
    jobnames = []

    def job(load, compute):
        jobs.append((load, compute))
        jobnames.append(getattr(compute, "__name__", "?"))
        if auto_defer[0] and getattr(compute, "__name__", "") != "<lambda>":
            drain(1)

    def wload(slot, views):
        assert len(views) <= 4
        for i, (dst, src) in enumerate(views):
            dma("pool", dst, src, w=[("w", slot, i)], group="w%d_%d" % (slot, i))
        for i in range(len(views), 4):
            pass

    def sl3(slot, nk, ncol, off=0):
        return wsl[slot][:, off:off + nk * ncol].rearrange("p (k n) -> p k n", n=ncol)

    def wsrc(w2d, c0, ncol):
        return w2d[:, c0:c0 + ncol].rearrange("(k p) n -> p k n", p=128)

    xtok = af32(0, 2 * 8 * T).rearrange("p (n d) -> p n d", d=D)
    for n in range(8):
        dma("sp" if n % 2 == 0 else "act", xtok[:, n, :], x_d[n * 128:(n + 1) * 128, :], w=[("xtok", n)])
    for fc in range(8):
        pst = ps0 if fc % 2 == 0 else ps1
        pc = "ps0" if fc % 2 == 0 else "ps1"
        for tb in range(8):
            tp(pst[:, tb * 128:(tb + 1) * 128], xtok[:, tb, fc * 128:(fc + 1) * 128], ident[:], r=[("xtok", tb), "ident"], w=[pc])
        cp("act" if fc % 2 == 0 else "dve", xT[:, fc, :], pst[:], r=[pc], w=[("xT", fc)])
    P.fence()

    act(condT[:], condT[:], AF.Silu, r=["condT"], w=["condT"])
    cp("dve", condb[:], condT[:], r=["condT"], w=["condb"])
    deferred = []
    auto_defer = [False]

    def ada_job(l, g):
        def ld(slot):
            wload(slot, [(sl3(slot, 8, 512), wsrc(w_ada_d[l], g * 512, 512))])

        def comp(slot):
            w3 = sl3(slot, 8, 512)
            pst, pc = nextps()
            for m in range(4):
                for k in range(8):
                    mm(pst[:, m:m + 1], w3[:, k, m * 128:(m + 1) * 128], condb[:, k:k + 1], k == 0, k == 7,
                       r=[("w", slot), "condb"], w=[pc])
            j0 = 4 * g
            tt("dve", modT[:, l, j0:j0 + 4], pst[:, 0:4], badaT[:, l, j0:j0 + 4], ALU.add, r=[pc, "badaT"], w=["modT"])
            if 8 <= j0 < 16:
                ts("dve", opsc[:, l, 0, j0 - 8:j0 - 4], modT[:, l, j0:j0 + 4], 1.0, None, ALU.add, r=["modT"], w=["opsc"])
            if 32 <= j0 < 40:
                ts("dve", opsc[:, l, 1, j0 - 32:j0 - 28], modT[:, l, j0:j0 + 4], 1.0, None, ALU.add, r=["modT"], w=["opsc"])
        return ld, comp

    for g in range(4):
        jobs.append(ada_job(0, g))
        jobnames.append("ada")
    for g in range(4, 12):
        deferred.append(ada_job(0, g))
    for g in range(12):
        deferred.append(ada_job(1, g))

    def drain(n):
        while n > 0 and deferred:
            jobs.append(deferred.pop(0))
            jobnames.append("ada_d")
            n -= 1

    def norm_mod(l, which):
        sh0 = 0 if which == 0 else 24

        def comp(slot):
            for c in range(8):
                if c % 2 == 0:
                    act(hT[:, c, :], xT[:, c, :], AF.Square, r=[("xT", c)], w=[("hT", c)])
                else:
                    tt("dve", hT[:, c, :], xT[:, c, :], xT[:, c, :], ALU.mult, r=[("xT", c)], w=[("hT", c)])
            for th in range(2):
                for c in range(8):
                    mm(ps0[:, th * 512:(th + 1) * 512], onesb[:], hT[:, c, th * 512:(th + 1) * 512], c == 0, c == 7,
                       r=[("hT", c), "onesb"], w=["ps0"])
            rstd_from(ps0[:], float(D), rstd[:], "ps0", "rstd")
            tmp = af32(ARENA - 2 * T * 2, 2 * T * 2).rearrange("p (b t) -> p b t", t=T)
            for c in range(8):
                tb_ = tmp[:, c % 2, :]
                stt("dve" if c % 2 == 0 else "pool", tb_, xT[:, c, :], opsc[:, l, which, c:c + 1], rstd[:], ALU.mult, ALU.mult,
                    r=[("xT", c), "opsc", "rstd"], w=[("nmt", c % 2)])
                act(hT[:, c, :], tb_, AF.Identity, r=[("nmt", c % 2), "modT"], w=[("hT", c)],
                    bias=modT[:, l, sh0 + c:sh0 + c + 1], scale=1.0)
        job(None, comp)

    def proj_fm(slot, w3, ncolchunk, m0, M, pst, pc, rhs_tile, rcell, nk):
        for th in range(2):
            for k in range(nk):
                mm(pst[0:M, th * 512:(th + 1) * 512], w3[:, k, m0:m0 + M], rhs_tile[:, k, th * 512:(th + 1) * 512], k == 0, k == nk - 1,
                   r=[("w", slot)] + [(rcell, k)], w=[pc])

    def proj_tm(slot, w3, n0, N, tb, pst, pc):
        for k in range(8):
            mm(pst[:, 0:N], hT[:, k, tb * 128:(tb + 1) * 128], w3[:, k, n0:n0 + N], k == 0, k == 7,
               r=[("w", slot), ("hT", k)], w=[pc])

    pp = [0]
    wide = [False]

    def nextps():
        if wide[0]:
            pp[0] = (pp[0] + 1) % 4
            return [(ps0, "ps0"), (ps1, "ps1"), (psSf, "psSf"), (psO, "psO")][pp[0]]
        pp[0] = (pp[0] + 1) % 2
        return (ps0, "ps0") if pp[0] else (ps1, "ps1")

    def set_wide(v):
        job(None, lambda slot: wide.__setitem__(0, v))

    def seg_out(dst_d, l, tb, width, src_ap, rc):
        seg, half = tb // 2, tb % 2
        dma("sp", dst_d[seg, l, half * 128:(half + 1) * 128, :], src_ap, r=[rc], group="out")

    H_QT = 0
    H_GT = H_QT + 4 * T
    H_VT = H_GT + 4 * T
    H_BT = H_VT + 8 * 512
    H_LF = H_BT + 4 * T
    H_E = H_LF + 4 * T
    H_QD = H_E + 4 * T
    H_KD = H_QD + 2 * T
    H_KDT = H_KD + 2 * T
    H_AT = H_KDT + 2 * T
    H_S = H_AT + 256
    H_SP = H_S + 512
    H_TMP = H_SP + 256
    H_LFT = H_TMP + 512
    H_COL = H_LFT + 512
    H_QIN = H_COL + 2 * 5 * 32
    H_KEND = H_QIN + 2 * T
    H_ZB = H_KEND + 2 * T
    H_E2 = H_ZB + 128
    H_END = H_E2 + 2 * T

    def hgrn(l):
        qT = abf(H_QT, 4 * T).rearrange("p (h t) -> p h t", t=T)
        gT = abf(H_GT, 4 * T).rearrange("p (h t) -> p h t", t=T)
        vtok = abf(H_VT, 8 * 512).rearrange("p (b n) -> p b n", n=512)
        bT = af32(H_BT, 4 * T).rearrange("p (d t) -> p d t", t=T)
        lfT = af32(H_LF, 4 * T).rearrange("p (d t) -> p d t", t=T)
        eT = af32(H_E, 4 * T).rearrange("p (d t) -> p d t", t=T)
        qd = abf(H_QD, 2 * T).rearrange("p (d t) -> p d t", t=T)
        kd = abf(H_KD, 2 * T).rearrange("p (d t) -> p d t", t=T)
        kdt = abf(H_KDT, 2 * T).rearrange("p (d b n) -> p d b n", d=2, n=128)
        AT = abf(H_AT, 256).rearrange("p (d n) -> p d n", n=128)
        S = af32(H_S, 512).rearrange("p (d n) -> p d n", n=128)
        Sp = abf(H_SP, 256).rearrange("p (d n) -> p d n", n=128)
        tmpS = af32(H_TMP, 512)
        lfts = [af32(H_LFT, 512), af32(H_TMP, 512)]
        cols = af32(H_COL, 2 * 5 * 32).rearrange("p (a d c) -> p a d c", a=5, d=2)
        qin = abf(H_QIN, 2 * T).rearrange("p (d t) -> p d t", t=T)
        kend = abf(H_KEND, 2 * T).rearrange("p (d t) -> p d t", t=T)
        zb = abf(H_ZB, 128)
        eT2 = af32(H_E2, 2 * T)

        def ld_q(slot):
            wload(slot, [(sl3(slot, 8, 512), wsrc(w_in_d[l], C_HQ, 512))])

        def c_q(slot):
            w3 = sl3(slot, 8, 512)
            for h in range(4):
                pst, pc = nextps()
                proj_fm(slot, w3, 512, h * 128, 128, pst, pc, hT, "hT", 8)
                act(qT[:, h, :], pst[:], AF.Copy, r=[pc], w=[("hqT", h)], scale=128 ** -0.5)
        job(ld_q, c_q)

        def ld_g(slot):
            wload(slot, [(sl3(slot, 8, 512), wsrc(w_in_d[l], C_HG, 512))])

        def c_g(slot):
            w3 = sl3(slot, 8, 512)
            for h in range(4):
                pst, pc = nextps()
                proj_fm(slot, w3, 512, h * 128, 128, pst, pc, hT, "hT", 8)
                act(gT[:, h, :], pst[:], AF.Silu, r=[pc], w=[("hgT", h)])
        job(ld_g, c_g)

        def ld_v(slot):
            wload(slot, [(sl3(slot, 8, 512), wsrc(w_in_d[l], C_HI, 512))])

        def c_v(slot):
            w3 = sl3(slot, 8, 512)
            for tb in range(8):
                pst, pc = nextps()
                proj_tm(slot, w3, 0, 512, tb, pst, pc)
                cp("act" if tb % 2 else "dve", vtok[:, tb, :], pst[:, 0:512], r=[pc], w=[("hvt", tb)])
        job(ld_v, c_v)

        for h in range(4):
            def ld_f(slot, h=h):
                w3 = sl3(slot, 8, 256)
                wload(slot, [(w3[:, :, 0:128], wsrc(w_in_d[l], C_HFF + h * 128, 128)),
                             (w3[:, :, 128:256], wsrc(w_in_d[l], C_HFB + h * 128, 128))])

            def c_f(slot, h=h):
                import os
                ksub = int(os.environ.get("KSUB", "99"))
                w3 = sl3(slot, 8, 256)
                neg = (l == 0)

                def s1_proj(tb):
                    pst, pc = nextps()
                    proj_tm(slot, w3, 0, 256, tb, pst, pc)
                    lft_ = lfts[tb % 2]
                    lfc = ("lft", tb % 2)
                    act(lft_[:], pst[:, 0:256], AF.Exp, r=[pc], w=[lfc], scale=-1.0)
                    if l == 0:
                        act(lft_[:], lft_[:], AF.Ln, r=[lfc], w=[lfc], bias=onec[:, 0:1], scale=1.0)
                    else:
                        l3 = lft_[:].rearrange("p (d n) -> p d n", n=128)
                        t3 = smallf[:, 0:256].rearrange("p (d n) -> p d n", n=128)
                        ts("dve", lft_[:], lft_[:], 1.0, None, ALU.add, r=[lfc], w=[lfc])
                        P.op("dve", lambda e, lft_=lft_: e.reciprocal(out=lft_[:], in_=lft_[:]), r=[lfc], w=[lfc])
                        ts("dve", t3, l3, -1.0, 1.0, ALU.mult, ALU.add, r=[lfc], w=["smallf"])
                        tt("dve", t3, t3, lbt[:, :, h * 128:(h + 1) * 128], ALU.mult, r=["smallf", "lbt"], w=["smallf"])
                        tt("dve", l3, l3, t3, ALU.add, r=[lfc, "smallf"], w=[lfc])
                        act(lft_[:], lft_[:], AF.Ln, r=[lfc], w=[lfc])

                def s1_cum(tb):
                    lft_ = lfts[tb % 2]
                    lfc = ("lft", tb % 2)
                    for dr in range(2):
                        ps_ = psS[dr]
                        pcs = "psS%d" % dr
                        mm(ps_[:, 0:256], lft_[:, dr * 128:(dr + 1) * 128], (trif if dr == 0 else trib)[:], True, True,
                           r=[lfc, "trif", "trib"], w=[pcs])
                        if neg:
                            ts("dve", bT[:, dr, tb * 128:(tb + 1) * 128], ps_[:, 0:128], -1.0, None, ALU.mult, r=[pcs], w=["bT"])
                            act(lfT[:, dr, tb * 128:(tb + 1) * 128], ps_[:, 128:256], AF.Copy, r=[pcs], w=["lfT"], scale=-1.0)
                        else:
                            cp("dve", bT[:, dr, tb * 128:(tb + 1) * 128], ps_[:, 0:128], r=[pcs], w=["bT"])
                            cp("act", lfT[:, dr, tb * 128:(tb + 1) * 128], ps_[:, 128:256], r=[pcs], w=["lfT"])

                s1_proj(0)
                for tb in range(8):
                    if tb + 1 < 8:
                        s1_proj(tb + 1)
                    s1_cum(tb)
                if ksub <= 1:
                    return
                memset("pool", zb, 0.0, w=["zb"])
                for th in range(2):
                    mm(psO[:, th * 512:(th + 1) * 512], zb, hT[:, 0, th * 512:(th + 1) * 512], True, True, r=["zb", ("hT", 0)], w=["psO"])
                obs = OB[:, 1, :, :]
                dbufs = [(abf(H_QD, T), abf(H_QD + T, T), abf(H_KD, T), abf(H_KD + T, T)),
                         (obs[:, 0, :], obs[:, 1, :], obs[:, 2, :], obs[:, 3, :])]

                def elem_part(dr):
                    qdH, qdX, kdH, kdX = dbufs[dr]
                    cqdH, cqdX, ckdH, ckdX = [(n_, dr) for n_ in ("qdH", "qdX", "kdH", "kdX")]
                    b3 = bT[:, dr, :].rearrange("p (c t) -> p c t", t=64)
                    b4 = bT[:, dr, :].rearrange("p (c t) -> p c t", t=32)
                    rbcol = cols[:, 0, dr, 0:16]
                    blast = cols[:, 1, dr, 0:16]
                    cp("dve", rbcol, b3[:, :, 31 if dr == 0 else 32], r=["bT"], w=["hcols"])
                    cp("dve", blast, b3[:, :, 63 if dr == 0 else 0], r=["bT"], w=["hcols"])
                    act(cols[:, 4, dr, 0:16], blast, AF.Exp, r=["hcols"], w=["hcols"])
                    e23 = eT2.rearrange("p (c t) -> p c t", t=64)
                    e13 = eT[:, 1, :].rearrange("p (c t) -> p c t", t=64)
                    blb = blast.rearrange("p (c o) -> p c o", o=1).to_broadcast([128, 16, 64])
                    rbb = rbcol.rearrange("p (c o) -> p c o", o=1).to_broadcast([128, 16, 64])
                    rh = smallf[:, 0:32]
                    rhb = rh.rearrange("p (c o) -> p c o", o=1).to_broadcast([128, 32, 32])
                    kT_ = lfT[:, dr, :]
                    act(kT_, kT_, AF.Exp, r=["lfT"], w=["lfT"])
                    tt("dve", e23, b3, blb, ALU.subtract, r=["bT", "hcols"], w=["eT2"])
                    tt("dve", e13, b3, rbb, ALU.subtract, r=["bT", "hcols"], w=[("eT", 1)])
                    ts("pool", kT_, kT_, -1.0, 1.0, ALU.mult, ALU.add, r=["lfT"], w=["lfT"])
                    act(eT2, eT2, AF.Exp, r=["eT2"], w=["eT2"], scale=-1.0)
                    ts("dve", eT[:, 0, :], eT[:, 1, :], 0.0, None, ALU.min, r=[("eT", 1)], w=[("eT", 0)])
                    ts("dve", eT[:, 1, :], eT[:, 1, :], 0.0, None, ALU.max, r=[("eT", 1)], w=[("eT", 1)])
                    act(eT[:, 0, :], eT[:, 0, :], AF.Exp, r=[("eT", 0)], w=[("eT", 0)])
                    act(eT[:, 1, :], eT[:, 1, :], AF.Exp, r=[("eT", 1)], w=[("eT", 1)], scale=-1.0)
                    tt("dve", kend[:, dr, :], kT_, eT2, ALU.mult, r=["lfT", "eT2"], w=[("kend", dr)])
                    tt("dve", qdX, qT[:, h, :], eT[:, 0, :], ALU.mult, r=[("hqT", h), ("eT", 0)], w=[cqdX])
                    tt("dve", kdX, kT_, eT[:, 1, :], ALU.mult, r=["lfT", ("eT", 1)], w=[ckdX])
                    act(eT2, bT[:, dr, :], AF.Exp, r=["bT"], w=["eT2"])
                    cp("dve", rh, b4[:, :, 16], r=["bT"], w=["smallf"])
                    tt("dve", b4, b4, rhb, ALU.subtract, r=["bT", "smallf"], w=["bT"])
                    ts("dve", bT[:, dr, :], bT[:, dr, :], 41.0, -41.0, ALU.min, ALU.max, r=["bT"], w=["bT"])
                    act(eT[:, 0, :], bT[:, dr, :], AF.Exp, r=["bT"], w=[("eT", 0)])
                    act(eT[:, 1, :], bT[:, dr, :], AF.Exp, r=["bT"], w=[("eT", 1)], scale=-1.0)
                    tt("dve", qin[:, dr, :], qT[:, h, :], eT2, ALU.mult, r=[("hqT", h), "eT2"], w=[("qin", dr)])
                    tt("dve", qdH, qT[:, h, :], eT[:, 0, :], ALU.mult, r=[("hqT", h), ("eT", 0)], w=[cqdH])
                    tt("dve", kdH, kT_, eT[:, 1, :], ALU.mult, r=["lfT", ("eT", 1)], w=[ckdH])

                def intra_part(dr):
                    qdH, qdX, kdH, kdX = dbufs[dr]
                    cqdH, cqdX, ckdH, ckdX = [(n_, dr) for n_ in ("qdH", "qdX", "kdH", "kdX")]
                    for tb in range(8):
                        pb = psS[tb % 2][:].bitcast(BF16)
                        pcs = "psS%d" % (tb % 2)
                        tp(pb[:, 0:128], kend[:, dr, tb * 128:(tb + 1) * 128], identb[:], r=[("kend", dr), "identb"], w=[pcs])
                        cp("act" if tb % 2 else "dve", kdt[:, dr, tb, :], pb[:, 0:128], r=[pcs], w=[("kdt", dr)])
                    for tb in range(8):
                        blk = slice(tb * 128, (tb + 1) * 128)
                        for x_, (kk, qq, kc_, qc_) in enumerate([(kdH, qdH, ckdH, cqdH), (kdX, qdX, ckdX, cqdX)]):
                            ps_ = psS[x_]
                            pcs = "psS%d" % x_
                            mo = (2 * x_ + dr) * 128
                            mm(ps_[:, 0:128], kk[:, blk], qq[:, blk], True, True, r=[kc_, qc_], w=[pcs])
                            tt("dve", AT[:, x_, :], ps_[:, 0:128], hmask[:, mo:mo + 128], ALU.mult, r=[pcs, "hmask"], w=[("AT", x_)])
                            mm(psO[:, blk], vtok[:, tb, h * 128:(h + 1) * 128], AT[:, x_, :], False, False,
                               r=[("hvt", tb), ("AT", x_)], w=["psO"])

                P.replay(P.capture(lambda: elem_part(0)))
                P.replay_interleaved(P.capture(lambda: intra_part(0)), P.capture(lambda: elem_part(1)))
                P.replay(P.capture(lambda: intra_part(1)))
                if ksub <= 4:
                    return
                for dr in range(2):
                    dma("sp", S[:, dr, :], st0_d[l, dr, h, :, :], w=[("S", dr)])
                ubuf = [[(psS[0], "psS0"), (ps0, "ps0")], [(psS[1], "psS1"), (ps1, "ps1")]]

                def emit_U(i, dr):
                    c = i if dr == 0 else 15 - i
                    tb, half = c // 2, c % 2
                    pu, puc = ubuf[dr][i % 2]
                    mm(pu[:, 0:128], kdt[half * 64:(half + 1) * 64, dr, tb, :], vtok[half * 64:(half + 1) * 64, tb, h * 128:(h + 1) * 128],
                       True, True, r=[("kdt", dr), ("hvt", tb)], w=[puc])

                for dr in range(2):
                    emit_U(0, dr)
                for i in range(16):
                    if i + 1 < 16:
                        for dr in range(2):
                            emit_U(i + 1, dr)
                    for dr in range(2):
                        c = i if dr == 0 else 15 - i
                        segstart = (c % 4 == 0 and c > 0) if dr == 0 else (c % 4 == 3 and c < 15)
                        segend = (c % 4 == 3) if dr == 0 else (c % 4 == 0)
                        if segstart:
                            ts("dve", S[:, dr, :], S[:, dr, :], keep[:, 0:1], None, ALU.mult, r=[("S", dr), "keep"], w=[("S", dr)])
                        act(Sp[:, dr, :], S[:, dr, :], AF.Copy, r=[("S", dr)], w=[("Sp", dr)])
                        mm(psO[:, c * 64:(c + 1) * 64], Sp[:, dr, :], qin[:, dr, c * 64:(c + 1) * 64], False, (i == 15 and dr == 1),
                           r=[("Sp", dr), ("qin", dr)], w=["psO"])
                        pu, puc = ubuf[dr][i % 2]
                        stt("dve", S[:, dr, :], S[:, dr, :], cols[:, 4, dr, c:c + 1], pu[:, 0:128], ALU.mult, ALU.add,
                            r=[("S", dr), puc, "hcols"], w=[("S", dr)])
                        if segend:
                            dma("sp", ost_d[c // 4, l, dr, h, :, :], S[:, dr, :], r=[("S", dr)], group="out")
                if ksub <= 5:
                    return
                on = af32(H_E, 2 * T)
                sq = abf(H_KD, T)
                act(sq, psO[:], AF.Square, r=["psO", ("kdH", 0)], w=[("kdH", 0)])
                for th in range(2):
                    mm(ps0[:, th * 512:(th + 1) * 512], onesb[:], sq[:, th * 512:(th + 1) * 512], True, True, r=[("kdH", 0), "onesb"], w=["ps0"])
                if ksub <= 8:
                    return
                rstd_from(ps0[:], 128.0, rstd[:], "ps0", "rstd")
                if ksub <= 9:
                    return
                tt("dve", on, psO[:], rstd[:], ALU.mult, r=["psO", "rstd", ("eT", 0)], w=[("eT", 0)])
                if ksub <= 10:
                    return
                ts("dve", on, on, gcols[:, l:l + 1], None, ALU.mult, r=[("eT", 0), "gcols"], w=[("eT", 0)])
                tt("dve", OB[:, 0, h, :], on, gT[:, h, :], ALU.mult, r=[("eT", 0), ("hgT", h)], w=[("OB", 0, h)])
            job(ld_f, c_f)

    A_Q = 0
    A_K = A_Q + 8 * T
    A_V = A_K + 8 * 1536
    A_V1 = A_V + 12 * 512
    A_PT = A_V1 + 2 * 12 * 128
    A_PSI = A_PT + 1024
    A_STG = A_PSI + 2048
    A_TMP = A_STG + 3072
    A_END = A_TMP + 4096
    assert A_END <= ARENA, A_END
    assert H_END <= ARENA, H_END

    def aviews():
        Q = abf(A_Q, 8 * T).rearrange("p (h t) -> p h t", t=T)
        K = abf(A_K, 8 * 1536).rearrange("p (h t) -> p h t", t=1536)
        V = abf(A_V, 12 * 512).rearrange("p (b h d) -> p b h d", h=8, d=64)
        V1 = abf(A_V1, 2 * 12 * 128).rearrange("p (s b n) -> p s b n", s=2, n=128)
        PT = abf(A_PT, 1024).rearrange("p (s n) -> p s n", n=512)
        PSI = abf(A_PSI, 1920).rearrange("p (s n) -> p s n", n=1920)
        STG = af32(A_STG, 3072).rearrange("p (b n) -> p b n", n=512)
        TMP = af32(A_TMP, 4096).rearrange("p (b n) -> p b n", n=T)
        return Q, K, V, V1, PT, PSI, STG, TMP

    def attn_masks(qm_d, nheads_k):
        Q, K, V, V1, PT, PSI, STG, TMP = aviews()
        dma("sp", Q[64:96, :, :], qm_d[:, :].rearrange("p (o t) -> p o t", o=1).to_broadcast([32, 8, T]), w=[("Q", h) for h in range(8)])
        dma("act", K[64:96, 0:nheads_k, :], kmask_d[:, :].rearrange("p (o t) -> p o t", o=1).to_broadcast([32, nheads_k, 1536]),
            w=[("K", h) for h in range(nheads_k)])
        memset("pool", V1[:, :, :, 64:128], 1.0, w=[("V1", 0), ("V1", 1)])

    def attn_core(branch, kvmap, krows, scale, mask_base, psi_fn=None):
        Q, K, V, V1, PT, PSI, STG, TMP = aviews()
        recip = TMP[0:64, 1, :]
        def kbs(qt):
            if psi_fn is None:
                return list(range(12))
            lo, hi = (0, 5) if qt == 0 else (2, 7)
            return [0, 1, 2, 3] + [4 + k for k in range(lo, hi + 1)]
        steps = [(h, qt, kb) for h in range(8) for qt in range(2) for kb in kbs(qt)]
        accs = [(psO, "psO"), (ps0, "ps0")]
        sbufs = [(psS[0], "psS0"), (psS[1], "psS1"), (ps1[:, 0:512], "ps1")]

        def emit_S(i):
            h, qt, kb = steps[i]
            kv = kvmap(h)
            if qt == 0 and kb == 0:
                vb = h % 2
                cp("pool", V1[:, vb, :, 0:64], V[:, :, kv, :], r=[("V", kv)], w=[("V1", vb)])
                if psi_fn is not None:
                    psi_fn(h)
            ps_, pcs = sbufs[i % 3]
            qs = slice(qt * 512, (qt + 1) * 512)
            bias = psi_fn is not None and kb >= 4
            mm(ps_[:], K[0:krows, kv, kb * 128:(kb + 1) * 128], Q[0:krows, h, qs], True, not bias,
               r=[("K", kv), ("Q", h)], w=[pcs])
            if bias:
                kbl = kb - 4
                st_ = (8 * qt - 2 * kbl + 14) * 64
                mm(ps_[:], identb[:], PSI[:, 0, st_:st_ + 512], False, True, r=[("PSI", 0), "identb"], w=[pcs])

        emit_S(0)
        emit_S(1)
        for i, (h, qt, kb) in enumerate(steps):
            if i + 2 < len(steps):
                emit_S(i + 2)
            sb = i % 2
            ps_, pcs = sbufs[i % 3]
            vb = h % 2
            acc, accc = accs[h % 2]
            qs = slice(qt * 512, (qt + 1) * 512)
            act(PT[:, sb, :], ps_[:], AF.Exp, r=[pcs], w=[("PT", sb)], scale=scale)
            kl = kbs(qt)
            mm(acc[:, qs], V1[:, vb, kb, :], PT[:, sb, :], kb == kl[0], kb == kl[-1], r=[("V1", vb), ("PT", sb)], w=[accc])
            if qt == 1 and kb == kl[-1]:
                P.op("dve", lambda e, acc=acc: e.reciprocal(out=recip, in_=acc[64:128, :]), r=[accc], w=["TMP1"])
                po = (h % 2) * 64
                tt("dve", OB[po:po + 64, branch, h // 2, :], acc[0:64, :], recip, ALU.mult, r=[accc, "TMP1"],
                   w=[("OB", branch, h // 2, h % 2)])

    def ctx_transposes(src_d2, ncol, dst_fn, name):
        Q, K, V, V1, PT, PSI, STG, TMP = aviews()
        for c in range((ncol + 127) // 128):
            w_ = min(128, ncol - c * 128)
            STGc = STG[:, 0, :].rearrange("p (n d) -> p n d", d=128)
            st = STGc[:, :, 0:w_]
            dma("sp", st, src_d2[:, c * 128:c * 128 + w_].rearrange("(n p) d -> p n d", p=128), w=["STG"])
            pst, pc = nextps()
            for n in range(4):
                tp(pst[0:w_, n * 128:(n + 1) * 128], STGc[:, n, 0:w_], ident[:], r=["STG", "ident"], w=[pc])
            dst_fn(c, pst, pc, w_)

    def rope_write(pst, pc, gcol, gl, dst_fn, nrm_n, TMP):
        sq = abf(A_PT, 1024)
        act(sq, pst[:], AF.Square, r=[pc], w=[("PT", 0), ("PT", 1)])
        for th in range(2):
            mm(psO[:, th * 512:(th + 1) * 512], blk64[:], sq[:, th * 512:(th + 1) * 512], True, True, r=[("PT", 0), ("PT", 1), "blk64"], w=["psO"])
        rstd_from(psO[:], 64.0, rstd[:], "psO", "rstd")
        qn = TMP[:, 0, :]
        tt("dve", qn, pst[:], rstd[:], ALU.mult, r=[pc, "rstd"], w=["TMP0"])
        ts("dve", qn, qn, gcols[:, gcol + gl:gcol + gl + 1], None, ALU.mult, r=["TMP0", "gcols"], w=["TMP0"])
        return qn

    def gqa(l):
        Q, K, V, V1, PT, PSI, STG, TMP = aviews()

        def c_pre(slot):
            attn_masks(qmf_d, 2)
            def dst(c, pst, pc, w_):
                cp("dve", K[0:64, 0, 0:512], pst[0:64, 0:512], r=[pc], w=[("K", 0)])
                cp("dve", K[0:64, 1, 0:512], pst[64:128, 0:512], r=[pc], w=[("K", 1)])
            ctx_transposes(ckg_d[l], 128, dst, "ckg")
            dma("pool", V[:, 0:4, 0:2, :], cvg_d[l].rearrange("(n p) (h d) -> p n h d", p=128, d=64), w=[("V", 0), ("V", 1)])
        job(None, c_pre)

        def rope_finish(qn, dsts, cells):
            qb = abf(A_PT, 1024)
            cp("act", qb, qn, r=["TMP0"], w=[("PT", 0), ("PT", 1)])
            for th in range(2):
                mm(psO[:, th * 512:(th + 1) * 512], pswap[:], qb[:, th * 512:(th + 1) * 512], True, True, r=[("PT", 0), ("PT", 1), "pswap"], w=["psO"])
            t2 = TMP[:, 1, :]
            tt("dve", t2, psO[:], sing[:], ALU.mult, r=["psO", "sing"], w=["TMP1"])
            tt("pool", qn, qn, cosg[:], ALU.mult, r=["TMP0", "cosg"], w=["TMP0"])
            for hh in range(2):
                tt("dve", dsts[hh], qn[hh * 64:(hh + 1) * 64, :], t2[hh * 64:(hh + 1) * 64, :], ALU.add, r=["TMP0", "TMP1"], w=[cells[hh]])

        def ld_q(slot):
            wload(slot, [(sl3(slot, 8, 512), wsrc(w_in_d[l], C_GQ, 512))])

        def c_q(slot):
            w3 = sl3(slot, 8, 512)
            for c in range(4):
                pst, pc = nextps()
                proj_fm(slot, w3, 512, c * 128, 128, pst, pc, hT, "hT", 8)
                qn = rope_write(pst, pc, 2, l, None, 64.0, TMP)
                rope_finish(qn, [Q[0:64, 2 * c, :], Q[0:64, 2 * c + 1, :]], [("Q", 2 * c), ("Q", 2 * c + 1)])
        job(ld_q, c_q)

        def ld_k(slot):
            w3 = sl3(slot, 8, 256)
            wload(slot, [(w3, wsrc(w_in_d[l], C_GK, 256))])

        def c_k(slot):
            w3 = sl3(slot, 8, 256)
            pst, pc = nextps()
            proj_fm(slot, w3, 256, 0, 128, pst, pc, hT, "hT", 8)
            qn = rope_write(pst, pc, 4, l, None, 64.0, TMP)
            rope_finish(qn, [K[0:64, 0, 512:1536], K[0:64, 1, 512:1536]], [("K", 0), ("K", 1)])
            dma("sp", gbc[:, 0:64], gqk_d[l:l + 1, :].partition_broadcast(128), w=["gbc"])
            kraw = TMP[:, 0, :]
            vst = TMP[:, 1, :]
            for tb in range(8):
                pst, pc = nextps()
                proj_tm(slot, w3, 0, 256, tb, pst, pc)
                cp("act", kraw[:, tb * 128:(tb + 1) * 128], pst[:, 0:128], r=[pc], w=["TMP0"])
                cp("dve", vst[:, tb * 128:(tb + 1) * 128], pst[:, 128:256], r=[pc], w=["TMP1"])
            sqa = STG[:, 0:2, :].rearrange("p b n -> p (b n)")
            act(sqa, kraw, AF.Square, r=["TMP0"], w=["STG", "STG1"])
            ss = smallf[:, 0:16]
            P.op("dve", lambda e: e.reduce_sum(out=ss, in_=sqa.rearrange("p (h d) -> p h d", d=64), axis=AX.X), r=["STG", "STG1"], w=["smallf"])
            rstd_from(ss, 64.0, smallf[:, 16:32], "smallf", "smallf")
            kn3 = sqa.rearrange("p (h d) -> p h d", d=64)
            tt("dve", kn3, kraw.rearrange("p (h d) -> p h d", d=64),
               smallf[:, 16:32].rearrange("p (h o) -> p h o", o=1).to_broadcast([128, 16, 64]), ALU.mult, r=["TMP0", "smallf"], w=["STG", "STG1"])
            tt("dve", kn3, kn3, gbc[:, 0:64].rearrange("p (o d) -> p o d", o=1).to_broadcast([128, 16, 64]), ALU.mult, r=["STG", "STG1", "gbc"], w=["STG", "STG1"])
            for hh_ in range(2):
                dma("sp", ogk_d[:, l, hh_ * 128:(hh_ + 1) * 128, :].rearrange("s p d -> p s d"), sqa.rearrange("p (s h d) -> p s h d", s=4, h=2)[:, :, hh_, :], r=["STG", "STG1"], group="out")
            for hh_ in range(2):
                dma("sp", ogv_d[:, l, hh_ * 128:(hh_ + 1) * 128, :].rearrange("s p d -> p s d"), vst.rearrange("p (s h d) -> p s h d", s=4, h=2)[:, :, hh_, :], r=["TMP1"], group="out")
            cp("dve", V[:, 4:12, 0:2, :], vst.rearrange("p (t h d) -> p t h d", h=2, d=64), r=["TMP1"], w=[("V", 0), ("V", 1)])
        job(ld_k, c_k)

        def c_core(slot):
            attn_core(1, lambda h: h // 4, 96, GQ_SCALE, 64)
        job(None, c_core)

    def na(l):
        Q, K, V, V1, PT, PSI, STG, TMP = aviews()

        def c_pre(slot):
            attn_masks(qmn_d, 8)

            def dst(c, pst, pc, w_):
                cp("dve", K[0:64, 2 * c, 0:512], pst[0:64, 0:512], r=[pc], w=[("K", 2 * c)])
                cp("dve", K[0:64, 2 * c + 1, 0:512], pst[64:128, 0:512], r=[pc], w=[("K", 2 * c + 1)])
            ctx_transposes(ckn_d[l], 512, dst, "ckn")
            dma("pool", V[:, 0:4, :, :], cvn_d[l].rearrange("(n p) (h d) -> p n h d", p=128, d=64), w=[("V", h) for h in range(8)])
            prow = af32(A_TMP, 256)
            memset("dve", prow[:, :], 0.0, w=["TMP0"])
            src = bass.AP(rpb_d.tensor, l * 8 * 15 * 31 + 30, [[31, 120], [-1, 31]])
            dma("sp", prow[0:120, 48:79], src, r=["TMP0"], w=["TMP0"], slow=True)
            dstb = bass.AP(bscr.tensor, l * 120 * 64 * 128, [[64 * 128, 120], [128, 64], [1, 128]])
            dma("sp", dstb, prow[0:120, :].rearrange("p (o n) -> p o n", o=1).to_broadcast([120, 64, 128]), r=["TMP0"], w=["bscr"], group="bscr")
        job(None, c_pre)

        def ld_q(slot):
            wload(slot, [(sl3(slot, 8, 512), wsrc(w_in_d[l], C_NQ, 512))])

        def c_q(slot):
            w3 = sl3(slot, 8, 512)
            for c in range(4):
                pst, pc = nextps()
                proj_fm(slot, w3, 512, c * 128, 128, pst, pc, hT, "hT", 8)
                act(Q[0:64, 2 * c, :], pst[0:64, :], AF.Copy, r=[pc], w=[("Q", 2 * c)], scale=NA_SCALE)
                ts("dve", Q[0:64, 2 * c + 1, :], pst[64:128, :], NA_SCALE, None, ALU.mult, r=[pc], w=[("Q", 2 * c + 1)])
        job(ld_q, c_q)

        def ld_k(slot):
            wload(slot, [(sl3(slot, 8, 512), wsrc(w_in_d[l], C_NK, 512))])

        def c_k(slot):
            w3 = sl3(slot, 8, 512)
            for c in range(4):
                pst, pc = nextps()
                proj_fm(slot, w3, 512, c * 128, 128, pst, pc, hT, "hT", 8)
                cp("act", K[0:64, 2 * c, 512:1536], pst[0:64, :], r=[pc], w=[("K", 2 * c)])
                cp("dve", K[0:64, 2 * c + 1, 512:1536], pst[64:128, :], r=[pc], w=[("K", 2 * c + 1)])
            for tb in range(8):
                pst, pc = nextps()
                proj_tm(slot, w3, 0, 512, tb, pst, pc)
                cp("act", STG[:, tb % 2, :], pst[:, 0:512], r=[pc], w=[["STG", "STG1"][tb % 2]])
                seg_out(onk_d, l, tb, 512, STG[:, tb % 2, :], ["STG", "STG1"][tb % 2])
        job(ld_k, c_k)

        def ld_v(slot):
            wload(slot, [(sl3(slot, 8, 512), wsrc(w_in_d[l], C_NV, 512))])

        def c_v(slot):
            w3 = sl3(slot, 8, 512)
            for tb in range(8):
                pst, pc = nextps()
                proj_tm(slot, w3, 0, 512, tb, pst, pc)
                cp("act", STG[:, 2, :], pst[:, 0:512], r=[pc], w=["STG2"])
                seg_out(onv_d, l, tb, 512, STG[:, 2, :], "STG2")
                cp("dve", V[:, 4 + tb, :, :], pst[:, 0:512].rearrange("p (h d) -> p h d", d=64), r=[pc], w=[("V", h) for h in range(8)])
        job(ld_v, c_v)

        def psi_build(h):
            sb = 0
            pf = TMP[:, 0, :]
            p3 = pf[:, 0:960].rearrange("p (r c) -> p r c", c=64)
            for par in range(2):
                base = (l * 120 + h * 15 + 14) * 64 * 128 + 63
                src = bass.AP(bscr.tensor, base, [[127, 64], [-64 * 128, 15], [1, 64]])
                dma("sp", p3[par * 64:(par + 1) * 64, :, :], src, r=["bscr"], w=["TMP0"])
            tt("dve", p3, p3, colm[:, :].rearrange("p (o c) -> p o c", o=1).to_broadcast([128, 15, 64]), ALU.add, r=["TMP0", "colm"], w=["TMP0"])
            memset("pool", PSI[:, sb, :], 0.0, w=[("PSI", sb)])
            for par in range(2):
                r0 = 7 + par
                cp("dve", PSI[par * 64:(par + 1) * 64, sb, r0 * 64:(r0 + 15) * 64], pf[par * 64:(par + 1) * 64, 0:960], r=["TMP0"], w=[("PSI", sb)])

        def c_core(slot):
            attn_core(2, lambda h: h, 96, 1.0, 64, psi_fn=psi_build)
        job(None, c_core)

    def mla(l):
        Q, K, V, V1, PT, PSI, STG, TMP = aviews()
        mqn = abf(A_PSI, 2 * T).rearrange("p (k t) -> p k t", t=T)
        ckvT = abf(A_V1, 1536)

        def c_pre(slot):
            Qm = abf(A_Q, 8 * T).rearrange("p (h t) -> p h t", t=T)
            dma("sp", Q[96:128, :, :], qmf_d[:, :].rearrange("p (o t) -> p o t", o=1).to_broadcast([32, 8, T]), w=[("Q", h) for h in range(8)])
            dma("act", K[96:128, :, :], kmask_d[:, :].rearrange("p (o t) -> p o t", o=1).to_broadcast([32, 8, 1536]), w=[("K", h) for h in range(8)])
            memset("pool", V1[:, 1, :, 64:128], 1.0, w=[("V1", 1)])

            def dst(c, pst, pc, w_):
                cp("dve", ckvT[:, 0:512], pst[:, 0:512], r=[pc], w=["ckvT"])
            ctx_transposes(cckv_d[l], 128, dst, "cckv")

            def dst2(c, pst, pc, w_):
                for h in range(8):
                    cp("dve" if h % 2 else "act", K[64:96, h, 0:512], pst[0:32, 0:512], r=[pc], w=[("K", h)])
            ctx_transposes(ckr_d[l], 32, dst2, "ckr")
        job(None, c_pre)

        def ld_a(slot):
            w3 = sl3(slot, 8, 512)
            wload(slot, [(w3[:, :, 0:416], wsrc(w_in_d[l], C_MQA, 416))])

        def c_a(slot):
            w3 = sl3(slot, 8, 512)
            qraw = TMP
            for c in range(2):
                pst, pc = nextps()
                proj_fm(slot, w3, 512, c * 128, 128, pst, pc, hT, "hT", 8)
                cp("dve", qraw[:, c, :], pst[:], r=[pc], w=["TMP%d" % c])
                act(mqn[:, c, :], pst[:], AF.Square, r=[pc], w=[("mqn", c)])
            for th in range(2):
                for c in range(2):
                    mm(psO[:, th * 512:(th + 1) * 512], onesb[:], mqn[:, c, th * 512:(th + 1) * 512], c == 0, c == 1, r=[("mqn", c), "onesb"], w=["psO"])
            rstd_from(psO[:], 256.0, rstd[:], "psO", "rstd")
            for c in range(2):
                stt("dve", mqn[:, c, :], qraw[:, c, :], gcols[:, 6 + 2 * l + c:7 + 2 * l + c], rstd[:], ALU.mult, ALU.mult,
                    r=["TMP%d" % c, "gcols", "rstd"], w=[("mqn", c)])
            pst, pc = nextps()
            proj_fm(slot, w3, 512, 256, 128, pst, pc, hT, "hT", 8)
            sq = abf(A_PT, 1024)
            act(sq, pst[:], AF.Square, r=[pc], w=[("PT", 0), ("PT", 1)])
            for th in range(2):
                mm(psO[:, th * 512:(th + 1) * 512], onesb[:], sq[:, th * 512:(th + 1) * 512], True, True, r=[("PT", 0), ("PT", 1), "onesb"], w=["psO"])
            rstd_from(psO[:], 128.0, rstd[:], "psO", "rstd")
            stt("dve", ckvT[:, 512:1536], pst[:], gcols[:, 10 + l:11 + l], rstd[:], ALU.mult, ALU.mult, r=[pc, "gcols", "rstd"], w=["ckvT"])
            pst, pc = nextps()
            proj_fm(slot, w3, 512, 320, 96, pst, pc, hT, "hT", 8)
            krb = abf(A_PT, 1024)
            cp("dve", krb[64:96, :], pst[64:96, :], r=[pc], w=[("PT", 0), ("PT", 1)])
            for th in range(2):
                mm(psO[0:32, th * 512:(th + 1) * 512], pswap[64:96, 64:96], krb[64:96, th * 512:(th + 1) * 512], True, True,
                   r=[("PT", 0), ("PT", 1), "pswap"], w=["psO"])
            t1 = TMP[64:96, 0, :]
            t2 = TMP[64:96, 1, :]
            tt("dve", t1, pst[64:96, :], csm[64:96, :], ALU.mult, r=[pc, "csm"], w=["TMP0"])
            tt("dve", t2, psO[0:32, :], csm[0:32, :], ALU.mult, r=["psO", "csm"], w=["TMP1"])
            tt("dve", t1, t1, t2, ALU.add, r=["TMP0", "TMP1"], w=["TMP0"])
            for h in range(8):
                cp("dve" if h % 2 else "act", K[64:96, h, 512:1536], t1, r=["TMP0"], w=[("K", h)])
            dma("sp", gbc[:, 0:128], mlkv_g_d[l:l + 1, :].partition_broadcast(128), w=["gbc"])
            craw = TMP[:, 0, :]
            krst = STG[:, 2, 0:256]
            for tb in range(8):
                pst, pc = nextps()
                proj_tm(slot, w3, 256, 160, tb, pst, pc)
                cp("act", craw[:, tb * 128:(tb + 1) * 128], pst[:, 0:128], r=[pc], w=["TMP0", "TMP0"])
                cp("dve", krst[:, tb * 32:(tb + 1) * 32], pst[:, 128:160], r=[pc], w=["STG2"])
            sqa = STG[:, 0:2, :].rearrange("p b n -> p (b n)")
            act(sqa, craw, AF.Square, r=["TMP0", "TMP0"], w=["STG", "STG1"])
            ss = smallf[:, 0:8]
            P.op("dve", lambda e: e.reduce_sum(out=ss, in_=sqa.rearrange("p (t d) -> p t d", d=128), axis=AX.X), r=["STG", "STG1"], w=["smallf"])
            rstd_from(ss, 128.0, smallf[:, 16:24], "smallf", "smallf")
            c3 = sqa.rearrange("p (t d) -> p t d", d=128)
            tt("dve", c3, craw.rearrange("p (t d) -> p t d", d=128),
               smallf[:, 16:24].rearrange("p (t o) -> p t o", o=1).to_broadcast([128, 8, 128]), ALU.mult, r=["TMP0", "TMP0", "smallf"], w=["STG", "STG1"])
            tt("dve", c3, c3, gbc[:, 0:128].rearrange("p (o d) -> p o d", o=1).to_broadcast([128, 8, 128]), ALU.mult, r=["STG", "STG1", "gbc"], w=["STG", "STG1"])
            for hh_ in range(2):
                dma("sp", ockv_d[:, l, hh_ * 128:(hh_ + 1) * 128, :].rearrange("s p d -> p s d"), sqa.rearrange("p (s h d) -> p s h d", s=4, h=2)[:, :, hh_, :], r=["STG", "STG1"], group="out")
            for hh_ in range(2):
                dma("sp", okr_d[:, l, hh_ * 128:(hh_ + 1) * 128, :].rearrange("s p d -> p s d"), krst.rearrange("p (s h d) -> p s h d", s=4, h=2)[:, :, hh_, :], r=["STG2"], group="out")
        job(ld_a, c_a)

        def ld_qb(slot):
            w3 = sl3(slot, 2, 768)
            wload(slot, [(w3, wsrc(wqb_d[l], 0, 768))])

        def c_qb(slot):
            w3 = sl3(slot, 2, 768)
            for h in range(8):
                pst, pc = nextps()
                proj_fm(slot, w3, 768, h * 96, 96, pst, pc, mqn, "mqn", 2)
                cp("act", Q[0:64, h, :], pst[0:64, :], r=[pc], w=[("Q", h)])
                qrb = abf(A_PT, 1024)
                cp("dve", qrb[64:96, :], pst[64:96, :], r=[pc], w=[("PT", 0), ("PT", 1)])
                for th in range(2):
                    mm(psO[0:32, th * 512:(th + 1) * 512], pswap[64:96, 64:96], qrb[64:96, th * 512:(th + 1) * 512], True, True,
                       r=[("PT", 0), ("PT", 1), "pswap"], w=["psO"])
                t1 = TMP[64:96, 0, :]
                t2 = TMP[64:96, 1, :]
                tt("dve", t1, pst[64:96, :], csm[64:96, :], ALU.mult, r=[pc, "csm"], w=["TMP0"])
                tt("dve", t2, psO[0:32, :], csm[0:32, :], ALU.mult, r=["psO", "csm"], w=["TMP1"])
                tt("dve", Q[64:96, h, :], t1, t2, ALU.add, r=["TMP0", "TMP1"], w=[("Q", h)])
        job(ld_qb, c_qb)

        def ld_kvb(slot):
            w3 = sl3(slot, 1, 1024)
            wload(slot, [(w3, wsrc(wkvb_d[l], 0, 1024))])

        def c_kvb(slot):
            w4 = wsl[slot][:, 0:1024].rearrange("p (h s d) -> p h s d", s=2, d=64)
            for h in range(8):
                for j in range(3):
                    sb = (h * 3 + j) % 2
                    ps_ = psS[sb]
                    pcs = "psS%d" % sb
                    mm(ps_[0:64, :], w4[:, h, 0, :], ckvT[:, j * 512:(j + 1) * 512], True, True, r=[("w", slot), "ckvT"], w=[pcs])
                    cp("act" if sb else "dve", K[0:64, h, j * 512:(j + 1) * 512], ps_[0:64, :], r=[pcs], w=[("K", h)])
            for kb in range(12):
                sb = kb % 2
                ps_ = psS[sb]
                pcs = "psS%d" % sb
                mm(ps_[:].rearrange("p (h d) -> p h d", d=64), ckvT[:, kb * 128:(kb + 1) * 128], w4[:, :, 1, :], True, True,
                   r=[("w", slot), "ckvT"], w=[pcs])
                cp("act" if sb else "dve", V[:, kb, :, :], ps_[:].rearrange("p (h d) -> p h d", d=64), r=[pcs], w=[("V", h) for h in range(8)])
        job(ld_kvb, c_kvb)

        def c_core(slot):
            memset("pool", V1[:, 0, :, 64:128], 1.0, w=[("V1", 0), "ckvT"])
            attn_core(3, lambda h: h, 128, ML_SCALE, 96)
        job(None, c_core)

    def merge_out(l):
        sg = abf(0, 4 * T).rearrange("p (n t) -> p n t", t=T)
        macc = af32(4 * T, 8 * T).rearrange("p (c t) -> p c t", t=T)
        mtmp = af32(12 * T, 4 * T).rearrange("p (b t) -> p b t", t=T)
        merged = abf(16 * T, 8 * T).rearrange("p (c t) -> p c t", t=T)
        for dcg in range(2):
            for n in range(4):
                def ld_g(slot, dcg=dcg, n=n):
                    wload(slot, [(sl3(slot, 8, 512), wsrc(w_in_d[l], C_GATE + n * 1024 + dcg * 512, 512))])

                def c_g(slot, dcg=dcg, n=n):
                    w3 = sl3(slot, 8, 512)
                    for m in range(4):
                        pst, pc = nextps()
                        proj_fm(slot, w3, 512, m * 128, 128, pst, pc, hT, "hT", 8)
                        act(sg[:, m, :], pst[:], AF.Sigmoid, r=[pc], w=[("sg", m)])
                job(ld_g, c_g)

                def ld_b(slot, dcg=dcg, n=n):
                    w3 = sl3(slot, 4, 512)
                    wload(slot, [(w3, wbr_d[l, n][:, dcg * 512:(dcg + 1) * 512].rearrange("(k p) c -> p k c", p=128))])

                def c_b(slot, dcg=dcg, n=n):
                    w3 = sl3(slot, 4, 512)
                    for m in range(4):
                        pst, pc = nextps()
                        for th in range(2):
                            for k in range(4):
                                mm(pst[:, th * 512:(th + 1) * 512], w3[:, k, m * 128:(m + 1) * 128], OB[:, n, k, th * 512:(th + 1) * 512],
                                   k == 0, k == 3, r=[("w", slot)] + obcells(n, k), w=[pc])
                        if n == 0:
                            tt("dve", macc[:, m, :], pst[:], sg[:, m, :], ALU.mult, r=[pc, ("sg", m)], w=[("macc", m)])
                        else:
                            tt("dve", mtmp[:, m % 2, :], pst[:], sg[:, m, :], ALU.mult, r=[pc, ("sg", m)], w=[("mtmp", m % 2)])
                            if n < 3:
                                tt("pool", macc[:, m, :], macc[:, m, :], mtmp[:, m % 2, :], ALU.add, r=[("macc", m), ("mtmp", m % 2)], w=[("macc", m)])
                            else:
                                tt("pool", merged[:, dcg * 4 + m, :], macc[:, m, :], mtmp[:, m % 2, :], ALU.add,
                                   r=[("macc", m), ("mtmp", m % 2)], w=[("merged", dcg * 4 + m)])
                job(ld_b, c_b)
        for g in range(2):
            def ld_o(slot, g=g):
                wload(slot, [(sl3(slot, 8, 512), wsrc(wout_d[l], g * 512, 512))])

            def c_o(slot, g=g):
                w3 = sl3(slot, 8, 512)
                for m in range(4):
                    mc = g * 4 + m
                    pst, pc = nextps()
                    proj_fm(slot, w3, 512, m * 128, 128, pst, pc, merged, "merged", 8)
                    stt("dve", xT[:, mc, :], pst[:], modT[:, l, 16 + mc:17 + mc], xT[:, mc, :], ALU.mult, ALU.add,
                        r=[pc, "modT", ("xT", mc)], w=[("xT", mc)])
            job(ld_o, c_o)

    def obcells(n, k):
        if n == 0:
            return [("OB", 0, k)]
        return [("OB", n, k, 0), ("OB", n, k, 1)]

    def ffn(l):
        uT = abf(0, 22 * T).rearrange("p (j t) -> p j t", t=T)
        silb = abf(22 * T, 2 * T).rearrange("p (b t) -> p b t", t=T)
        for jp in range(11):
            def ld(slot, jp=jp):
                w3 = sl3(slot, 8, 512)
                wload(slot, [(w3[:, :, 0:256], wsrc(wfi_d[l], jp * 256, 256)), (w3[:, :, 256:512], wsrc(wfi_d[l], FFN + jp * 256, 256))])

            def comp(slot, jp=jp):
                w3 = sl3(slot, 8, 512)
                for jj in range(2):
                    j = jp * 2 + jj
                    (pa, pac), (pg, pgc) = [((ps0, "ps0"), (ps1, "ps1")), ((psSf, "psSf"), (psO, "psO"))][jj]
                    silj = silb[:, jj, :]
                    proj_fm(slot, w3, 512, 256 + jj * 128, 128, pg, pgc, hT, "hT", 8)
                    proj_fm(slot, w3, 512, jj * 128, 128, pa, pac, hT, "hT", 8)
                    act(silj, pg[:], AF.Silu, r=[pgc], w=[("sil", jj)])
                    tt("dve", uT[:, j, :], pa[:], silj, ALU.mult, r=[pac, ("sil", jj)], w=[("uT", j)])
            job(ld, comp)
        tiles = [(ps0, "ps0"), (ps1, "ps1"), (psSf, "psSf"), (psO, "psO")]
        for mp in range(4):
            for kh in range(2):
                def ld(slot, mp=mp, kh=kh):
                    w3 = sl3(slot, 11, 256)
                    wload(slot, [(w3, wfo_d[l][kh * 1408:(kh + 1) * 1408, mp * 256:(mp + 1) * 256].rearrange("(k p) c -> p k c", p=128))])

                def comp(slot, mp=mp, kh=kh):
                    w3 = sl3(slot, 11, 256)
                    for mm_ in range(2):
                        m = mp * 2 + mm_
                        pst, pc = tiles[(mp % 2) * 2 + mm_]
                        for th in range(2):
                            for k in range(11):
                                kk = kh * 11 + k
                                mm(pst[:, th * 512:(th + 1) * 512], w3[:, k, mm_ * 128:(mm_ + 1) * 128], uT[:, kk, th * 512:(th + 1) * 512],
                                   kk == 0, kk == 21, r=[("w", slot), ("uT", kk)], w=[pc])
                        if kh == 1:
                            stt("dve", xT[:, m, :], pst[:], modT[:, l, 40 + m:41 + m], xT[:, m, :], ALU.mult, ALU.add,
                                r=[pc, "modT", ("xT", m)], w=[("xT", m)])
                job(ld, comp)

    def fence_job():
        job(None, lambda slot: P.fence())

    for l in range(L):
        if l == 1:
            drain(100)
        norm_mod(l, 0)
        auto_defer[0] = True
        hgrn(l)
        fence_job()
        gqa(l)
        fence_job()
        na(l)
        fence_job()
        mla(l)
        auto_defer[0] = False
        drain(100 if l == 0 and len(deferred) > 12 else 0)
        fence_job()
        set_wide(True)
        merge_out(l)
        set_wide(False)
        norm_mod(l, 1)
        ffn(l)
    fence_job()

    def final(slot):
        for c in range(8):
            act(hT[:, c, :], xT[:, c, :], AF.Square, r=[("xT", c)], w=[("hT", c)])
        for th in range(2):
            for c in range(8):
                mm(ps0[:, th * 512:(th + 1) * 512], onesb[:], hT[:, c, th * 512:(th + 1) * 512], c == 0, c == 7, r=[("hT", c), "onesb"], w=["ps0"])
        rstd_from(ps0[:], float(D), rstd[:], "ps0", "rstd")
        ytok = af32(0, 2 * 8 * T).rearrange("p (n d) -> p n d", d=D)
        yn = af32(2 * 8 * T, 2 * 8 * T).rearrange("p (b t) -> p b t", t=T)
        for c in range(8):
            stt("dve", yn[:, c, :], xT[:, c, :], fgc[:, c:c + 1], rstd[:], ALU.mult, ALU.mult, r=[("xT", c), "fgc", "rstd"], w=[("yn", c)])
        for tb in range(8):
            pst, pc = (ps1, "ps1") if tb % 2 == 0 else (psO, "psO")
            for c in range(8):
                tp(pst[:, c * 128:(c + 1) * 128], yn[:, c, tb * 128:(tb + 1) * 128], ident[:], r=[("yn", c), "ident"], w=[pc])
            cp("act" if tb % 2 else "dve", ytok[:, tb, :], pst[:], r=[pc], w=[("ytok", tb)])
            dma("sp" if tb % 2 == 0 else "act", y_d[tb * 128:(tb + 1) * 128, :], ytok[:, tb, :], r=[("ytok", tb)], group="out")
    job(None, final)

    wj = [i for i, (ld, _) in enumerate(jobs) if ld is not None]
    widx_of = {ji: k for k, ji in enumerate(wj)}
    lp = 0
    done_w = 0
    import os
    kstop = int(os.environ.get("KSTOP", "100000"))
    if kstop < len(jobs):
        print("JOBS:", [(i, n) for i, n in enumerate(jobnames)][:kstop + 2])
    for ji, (ld, comp) in enumerate(jobs):
        if ji >= kstop:
            break
        while lp < len(wj) and lp < done_w + NSLOT and wj[lp] < kstop:
            jobs[wj[lp]][0](lp % NSLOT)
            lp += 1
        if ld is not None:
            comp(widx_of[ji] % NSLOT)
            done_w += 1
        else:
            comp(None)

    if os.environ.get("KDUMP"):
        dbg = nc.dram_tensor("dbg", [128, 16, T], F32, kind="ExternalOutput").ap()
        P.fence()
        dma("pool", dbg[:, :, :], OB[:, :, :, :].rearrange("p n c t -> p (n c) t"), r=[], group="out")
    print("SBUF bytes remaining:", nc.sbuf_bytes_remaining)
    P.emit(nc, stack)
    stack.close()
    return nc


def _consts(is_sample):
    c = {}
    c["ident"] = np.eye(128, dtype=np.float32)
    b = np.zeros((128, 128), np.float32)
    b[0:64, 0:64] = 1
    b[64:128, 64:128] = 1
    c["blk64"] = b
    p = np.zeros((128, 128), np.float32)
    for i in range(64):
        p[2 * i + 1, 2 * i] = -1.0
        p[2 * i, 2 * i + 1] = 1.0
    c["pswap"] = p
    s = np.arange(128)
    same = (s[:, None] // 64) == (s[None, :] // 64)
    trif = (same & (s[:, None] <= s[None, :])).astype(np.float32)
    trib = (same & (s[:, None] >= s[None, :])).astype(np.float32)
    c["trif"] = np.concatenate([trif, np.eye(128, dtype=np.float32)], axis=1)
    c["trib"] = np.concatenate([trib, np.eye(128, dtype=np.float32)], axis=1)
    sameh = (s[:, None] // 32) == (s[None, :] // 32)
    early = (s % 64) < 32
    mh_f = (sameh & (s[:, None] <= s[None, :])).astype(np.float32)
    mh_b = (sameh & (s[:, None] >= s[None, :])).astype(np.float32)
    mx_f = (same & early[:, None] & (~early)[None, :]).astype(np.float32)
    mx_b = (same & (~early)[:, None] & early[None, :]).astype(np.float32)
    c["hmask"] = np.concatenate([mh_f, mh_b, mx_f, mx_b], axis=1)
    km = np.zeros((32, 1536), np.float32)
    km[16, 0:512] = 1
    t = np.arange(1024)
    km[t // 64, 512 + t] = 1
    c["kmask"] = km.astype(ml_dtypes.bfloat16)
    qf = np.zeros((32, 1024), np.float32)
    qn = np.zeros((32, 1024), np.float32)
    rq = t // 64
    if is_sample:
        start = np.clip(rq - 4, 0, 8)
        for j in range(16):
            ok = (j >= start) & (j < start + 8)
            qn[j, ~ok] = -BIG
    else:
        for j in range(16):
            ok = (rq // 4) == (j // 4)
            qf[j, ~ok] = -BIG
        qf[16, :] = -BIG
        qn[:] = qf
    c["qmask_full"] = qf.astype(ml_dtypes.bfloat16)
    c["qmask_na"] = qn.astype(ml_dtypes.bfloat16)
    cm = np.zeros((128, 64), np.float32)
    if is_sample:
        col = np.arange(64)
        cs = np.clip(col - 8, 0, 48)
        ok = (col[None, :] >= cs[:, None]) & (col[None, :] < cs[:, None] + 16)
        m = np.where(ok.T, 0.0, -BIG).astype(np.float32)
        cm[0:64] = m
        cm[64:128] = m
    c["colmask"] = cm

    def rope(rot_dim):
        nf = rot_dim // 4
        inv = 10000.0 ** (-np.arange(nf, dtype=np.float32) / nf)
        row = (t // 64).astype(np.float32)[:, None]
        colv = (t % 64).astype(np.float32)[:, None]
        ang = np.concatenate([row * inv, colv * inv], axis=-1)
        return np.cos(ang).astype(np.float32), np.sin(ang).astype(np.float32)

    cg = np.ones((128, 1024), np.float32)
    sg = np.zeros((128, 1024), np.float32)
    csm = np.zeros((128, 1024), np.float32)
    csm[64:96] = 1.0
    if is_sample:
        co, si = rope(64)
        cg[:] = np.tile(np.repeat(co.T, 2, axis=0), (2, 1))
        sg[:] = np.tile(np.repeat(si.T, 2, axis=0), (2, 1))
        co, si = rope(32)
        csm[64:96] = np.repeat(co.T, 2, axis=0)
        csm[0:32] = np.repeat(si.T, 2, axis=0)
    c["cos_g"] = cg
    c["sin_g"] = sg
    c["cossin_m"] = csm
    c["keep"] = np.full((128, 1), 1.0 if is_sample else 0.0, np.float32)
    return c


_NC_CACHE = {}


def kernel(**inp):
    f = lambda a: np.ascontiguousarray(np.asarray(a, dtype=np.float32))
    inp = {k: f(v) for k, v in inp.items()}
    if "nc" not in _NC_CACHE:
        _NC_CACHE["nc"] = build()
    nc = _NC_CACHE["nc"]
    shared = {k: inp[k] for k in ["w_ada", "b_ada", "w_in", "hg_lb_logits", "hg_gain", "gq_q_gain", "gq_k_gain",
                                  "ml_q_a_gain", "ml_kv_a_gain", "ml_w_q_b", "ml_w_kv_b", "w_branch", "w_out",
                                  "w_ffn_in", "w_ffn_out"]}
    shared["final_gain"] = inp["final_gain"].reshape(1, D)
    cs, cp_ = _consts(True), _consts(False)
    in_maps = []
    for core in range(8):
        m = dict(shared)
        if core < 4:
            b = core
            m.update(cs)
            m["x"] = inp["x_sample"][b]
            m["cond"] = inp["c"][b:b + 1]
            m["na_rpb"] = inp["na_rpb"]
            m["state0"] = inp["state_hgrn"][b]
            m["ck_gqa"] = inp["cache_gqa_k"][b].reshape(L, 512, 128)
            m["cv_gqa"] = inp["cache_gqa_v"][b].reshape(L, 512, 128)
            m["ck_na"] = inp["cache_na_k"][b].reshape(L, 512, 512)
            m["cv_na"] = inp["cache_na_v"][b].reshape(L, 512, 512)
            m["c_ckv"] = inp["cache_mla_ckv"][b]
            m["c_kr"] = inp["cache_mla_krope"][b]
        else:
            j = core - 4
            m.update(cp_)
            m["x"] = inp["x_prompt"][4 * j:4 * j + 4].reshape(T, D)
            m["cond"] = inp["c_ctx"].reshape(1, D)
            m["na_rpb"] = np.zeros((L, 8, 15, 31), np.float32)
            m["state0"] = np.zeros((L, 2, 4, 128, 128), np.float32)
            m["ck_gqa"] = np.zeros((L, 512, 128), np.float32)
            m["cv_gqa"] = np.zeros((L, 512, 128), np.float32)
            m["ck_na"] = np.zeros((L, 512, 512), np.float32)
            m["cv_na"] = np.zeros((L, 512, 512), np.float32)
            m["c_ckv"] = np.zeros((L, 512, 128), np.float32)
            m["c_kr"] = np.zeros((L, 512, 32), np.float32)
        in_maps.append({k: np.ascontiguousarray(v) for k, v in m.items()})
    import os
    kc = os.environ.get("KCORES")
    if kc:
        ids = [int(v) for v in kc.split(",")]
        res = run_bass_kernel_spmd(nc, [in_maps[i] for i in ids], core_ids=list(range(len(ids))), trace=bool(os.environ.get("KTRACE")))
        print("EXEC_NS", res.exec_time_ns)
        return {i: r for i, r in zip(ids, res.results)}
    res = run_bass_kernel_spmd(nc, in_maps, core_ids=list(range(8)))
    R = res.results
    y_sample = np.stack([R[b]["y"] for b in range(4)], 0)
    y_prompt = np.concatenate([R[4 + j]["y"].reshape(4, 256, D) for j in range(4)], 0)
    cat = lambda name, shp: np.concatenate([R[4 + j][name] for j in range(4)], 0).reshape(shp)
    return (y_prompt.astype(np.float32), y_sample.astype(np.float32),
            cat("o_state", (16, L, 2, 4, 128, 128)),
            cat("o_gqk", (16, L, 256, 2, 64)), cat("o_gqv", (16, L, 256, 2, 64)),
            cat("o_nak", (16, L, 256, 8, 64)), cat("o_nav", (16, L, 256, 8, 64)),
            cat("o_ckv", (16, L, 256, 128)), cat("o_kr", (16, L, 256, 32)))
```
